# Optimizing a Trainium2 kernel written in Bass

```python
import math
import jax, jax.numpy as jnp
from jax import lax
import numpy as np

D_MODEL = 1024
BATCH = 2
SEQ = 8192
DEPTH = 1

D_PLE = 256
D_MIX = D_MODEL
D_CONV = D_MIX // 2
D_SGU = D_MIX - D_CONV
CONV_GROUPS = 8
CONV_WIDTH = 31
N_SGU_HEADS = 8
SGU_HEAD_DIM = D_SGU // N_SGU_HEADS
CHUNK = 128
LN_EPS = 1e-5
ALPHA = (2 * DEPTH) ** 0.25
BETA = (8 * DEPTH) ** -0.25
D_IN = 3 * D_CONV + 3 * D_SGU

kernel_name = "hybrid_conformer_conv_chunked_sgu_deepnorm"


def _layer_norm(x, g, b):
    xf = x.astype(jnp.float32)
    mu = jnp.mean(xf, axis=-1, keepdims=True)
    var = jnp.mean(jnp.square(xf - mu), axis=-1, keepdims=True)
    y = (xf - mu) * lax.rsqrt(var + LN_EPS)
    return (y * g.astype(jnp.float32) + b.astype(jnp.float32)).astype(x.dtype)


def _causal_depthwise_conv(a, w, b):
    out = lax.conv_general_dilated(
        a, w[:, None, :].astype(a.dtype),
        window_strides=(1,),
        padding=[(CONV_WIDTH - 1, 0)],
        dimension_numbers=("NWC", "WIO", "NWC"),
        feature_group_count=a.shape[-1])
    return out + b


def _conformer_conv_branch(a_val, a_gate, a_z, conv_w, conv_b, ln_g, ln_b):
    a = a_val * jax.nn.sigmoid(a_gate)
    a = _causal_depthwise_conv(a, conv_w, conv_b)
    a = jax.nn.silu(_layer_norm(a, ln_g, ln_b))
    return a * jax.nn.silu(a_z)


def _chunked_sgu_branch(b_u, b_v, b_z, ln_g, ln_b, w_s, b_s):
    bsz, seq, _ = b_u.shape
    u = jax.nn.gelu(b_u, approximate=False)
    v = _layer_norm(jax.nn.gelu(b_v, approximate=False), ln_g, ln_b)
    vh = v.reshape(bsz, seq // CHUNK, CHUNK, N_SGU_HEADS, SGU_HEAD_DIM)
    mask = jnp.tril(jnp.ones((CHUNK, CHUNK), dtype=bool))
    w = jnp.where(mask[None], w_s, jnp.zeros((), w_s.dtype))
    sv = jnp.einsum("hts,bcshd->bcthd", w, vh)
    sv = sv + jnp.transpose(b_s)[None, None, :, :, None]
    sv = sv.reshape(bsz, seq, D_SGU)
    return (u * sv) * jax.nn.silu(b_z)


def setup_inputs(seed: int = 0) -> dict:
    key = jax.random.key(seed)
    ks = jax.random.split(key, 20)
    f32 = jnp.float32
    nrm = lambda k, shape, s: jax.random.normal(k, shape, f32) * s
    return {
        "x": nrm(ks[0], (BATCH, SEQ, D_MODEL), 1.0),
        "p": nrm(ks[1], (DEPTH, BATCH, SEQ, D_PLE), 1.0),
        "ln_emb_g": 1.0 + nrm(ks[2], (D_MODEL,), 0.02),
        "ln_emb_b": nrm(ks[3], (D_MODEL,), 0.02),
        "w_in": nrm(ks[4], (DEPTH, D_MODEL, D_IN), D_MODEL ** -0.5),
        "conv_w": nrm(ks[5], (DEPTH, CONV_WIDTH, D_CONV), CONV_WIDTH ** -0.5),
        "conv_b": nrm(ks[6], (DEPTH, D_CONV), 0.02),
        "conv_ln_g": 1.0 + nrm(ks[7], (DEPTH, D_CONV), 0.02),
        "conv_ln_b": nrm(ks[8], (DEPTH, D_CONV), 0.02),
        "sgu_ln_g": 1.0 + nrm(ks[9], (DEPTH, D_SGU), 0.02),
        "sgu_ln_b": nrm(ks[10], (DEPTH, D_SGU), 0.02),
        "w_s": nrm(ks[11], (DEPTH, N_SGU_HEADS, CHUNK, CHUNK), CHUNK ** -0.5),
        "b_s": 1.0 + nrm(ks[12], (DEPTH, N_SGU_HEADS, CHUNK), 0.02),
        "w_out": nrm(ks[13], (DEPTH, D_MIX, D_MODEL), BETA * D_MIX ** -0.5),
        "post_ln_g": 1.0 + nrm(ks[14], (DEPTH, D_MODEL), 0.02),
        "post_ln_b": nrm(ks[15], (DEPTH, D_MODEL), 0.02),
        "w_ple": nrm(ks[16], (DEPTH, D_PLE, D_MODEL), D_PLE ** -0.5),
        "w_ple_gate": nrm(ks[17], (DEPTH, D_MODEL, D_MODEL), D_MODEL ** -0.5),
        "b_ple_gate": nrm(ks[18], (DEPTH, D_MODEL), 0.02),
    }


def reference(x, p, ln_emb_g, ln_emb_b, w_in, conv_w, conv_b, conv_ln_g, conv_ln_b,
              sgu_ln_g, sgu_ln_b, w_s, b_s, w_out, post_ln_g, post_ln_b,
              w_ple, w_ple_gate, b_ple_gate):
    splits = [D_CONV, 2 * D_CONV, 3 * D_CONV, 3 * D_CONV + D_SGU, 3 * D_CONV + 2 * D_SGU]
    h = _layer_norm(x, ln_emb_g, ln_emb_b)
    for i in range(DEPTH):
        proj = jnp.einsum("bsd,de->bse", h, w_in[i])
        a_val, a_gate, a_z, b_u, b_v, b_z = jnp.split(proj, splits, axis=-1)
        y_a = _conformer_conv_branch(a_val, a_gate, a_z, conv_w[i], conv_b[i],
                                     conv_ln_g[i], conv_ln_b[i])
        y_b = _chunked_sgu_branch(b_u, b_v, b_z, sgu_ln_g[i], sgu_ln_b[i], w_s[i], b_s[i])
        y = jnp.concatenate([y_a, y_b], axis=-1)
        mix = jnp.einsum("bse,ed->bsd", y, w_out[i])
        h = _layer_norm(ALPHA * h + mix, post_ln_g[i], post_ln_b[i])
        pe = jnp.einsum("bsk,kd->bsd", p[i], w_ple[i])
        gate = jax.nn.sigmoid(jnp.einsum("bsd,de->bse", h, w_ple_gate[i]) + b_ple_gate[i])
        h = h + gate * pe
    return h
```

```python
import numpy as np
from contextlib import ExitStack
import concourse.bass as bass
import concourse.mybir as mybir
from concourse.bass_utils import run_bass_kernel_spmd

F32 = mybir.dt.float32
BF16 = mybir.dt.bfloat16
AF = mybir.ActivationFunctionType
ALU = mybir.AluOpType

NCORES = 8
D = 1024
NT = 2048
NSUB = NT // 128
ST = 512
NST = NT // ST
DPLE = 256
ALPHA = float(2.0 ** 0.25)
EPS = 1e-5
A_VAL, A_GATE, A_Z, B_U, B_V, B_Z = 0, 512, 1024, 1536, 2048, 2560
NTAP = 31
C_GE, C_BE, C_CB, C_LG, C_LB, C_HM, C_M0, C_M1, NCOL = 0, 8, 16, 20, 24, 28, 29, 30, 32


class Op:
    __slots__ = ("eng", "fn", "dma", "deps", "signaled", "tok", "clock", "c", "tbl", "idx", "end", "nin", "users", "rdy")

    def __init__(self, eng, fn, dma):
        self.eng, self.fn, self.dma = eng, fn, dma
        self.deps = ()
        self.signaled = False
        self.tok = None
        self.clock = None


DEF_COST = {"pe": 0.27, "act": 0.8, "dve": 0.8, "pool": 1.5, "sp": 0.4}


class Em:
    def __init__(self, nc):
        self.nc = nc
        self.ops = []
        self.last_w = {}
        self.readers = {}
        self.h = {"pe": nc.tensor, "act": nc.scalar, "dve": nc.vector, "pool": nc.gpsimd, "sp": nc.sync}

    @staticmethod
    def cost(eng, fn, n, dma):
        names = fn.__code__.co_names
        if dma is not None:
            return 2.0 + n * 128 * 4 / 150e3
        if eng == "pe":
            return 0.1 if "transpose" in names else 0.02 + n * 0.00042
        if eng == "act":
            return 0.35 + n / 1400.0
        if eng == "dve":
            if "bn_stats" in names:
                return 0.06 + n / 800.0
            if "bn_aggr" in names:
                return 0.2
            return 0.1 + n / 870.0
        if eng == "pool":
            if "tensor_tensor" in names:
                return 0.3 + n * 0.002
            return 0.5
        return 0.4

    def op(self, eng, fn, reads=(), writes=(), dma=None, c=None, tbl=None, n=512):
        o = Op(eng, fn, dma)
        o.c = self.cost(eng, fn, n, dma) if c is None else c
        if eng == "pe" and tbl is None:
            tbl = "full"
        if eng == "act" and tbl is None:
            names = fn.__code__.co_consts
        o.tbl = tbl
        deps = []
        seen = set()

        def add(d):
            if d is not None and id(d) not in seen:
                seen.add(id(d))
                deps.append(d)

        for k in reads:
            add(self.last_w.get(k))
        for k in writes:
            add(self.last_w.get(k))
            for r in self.readers.get(k, ()):
                add(r)
        o.deps = deps
        for k in reads:
            self.readers.setdefault(k, []).append(o)
        for k in writes:
            self.last_w[k] = o
            self.readers[k] = []
        self.ops.append(o)
        return o

    def schedule(self):
        import heapq
        ops = self.ops
        for i, o in enumerate(ops):
            o.idx = i
            o.users = []
            o.nin = 0
            o.rdy = 0.0
            o.end = None
        for o in ops:
            for d in o.deps:
                d.users.append(o)
                o.nin += 1
        import os
        POL = os.environ.get("SCHED_POL", "fifo")
        SLACK = float(os.environ.get("SCHED_SLACK", "0.3"))
        bl = [0.0] * len(ops)
        for o in reversed(ops):
            m = 0.0
            for u in o.users:
                if bl[u.idx] > m:
                    m = bl[u.idx]
            bl[o.idx] = o.c + m
        ready = {e: [] for e in self.h}
        for o in ops:
            if o.nin == 0:
                heapq.heappush(ready[o.eng], (o.idx, o))
        free_at = {e: 0.0 for e in self.h}
        cur_tbl = [None]
        cur_mode = [None]
        HOP = float(os.environ.get("SCHED_HOP", "0.15"))
        SCALE = {"pe": float(os.environ.get("SCHED_PE", "1.1")), "act": float(os.environ.get("SCHED_ACT", "1.0")),
                 "dve": float(os.environ.get("SCHED_DVE", "1.15")), "pool": float(os.environ.get("SCHED_POOL", "1.0")), "sp": 1.0}
        order = []
        n = len(ops)
        LOOK = int(os.environ.get("SCHED_LOOK", "20"))
        while len(order) < n:
            best = None
            for e, hp in ready.items():
                if not hp:
                    continue
                cands = heapq.nsmallest(LOOK, hp)
                for idx, o in cands:
                    st = max(free_at[e], o.rdy)
                    pen = 0.0
                    if e == "act" and o.tbl is not None and o.tbl != cur_tbl[0]:
                        pen = 1.3
                    if e == "pe" and o.tbl is not None and o.tbl != cur_mode[0]:
                        pen = 0.25
                    if POL == "bl":
                        key = (round((st + pen) / SLACK), 1 if (e == "pool" and o.dma is not None) else 0, -bl[idx], idx)
                    else:
                        key = (st + pen, 1 if (e == "pool" and o.dma is not None) else 0, idx)
                    if best is None or key < best[0]:
                        best = (key, o, st + pen)
            _, o, st = best
            e = o.eng
            ready[e] = [(i, x) for (i, x) in ready[e] if x is not o]
            heapq.heapify(ready[e])
            if o.dma is not None:
                free_at[e] = st + (1.1 if e == "pool" else 0.45)
                o.end = st + o.c
            else:
                free_at[e] = st + o.c * SCALE[e]
                o.end = free_at[e]
            if e == "act" and o.tbl is not None:
                cur_tbl[0] = o.tbl
            if e == "pe" and o.tbl is not None:
                cur_mode[0] = o.tbl
            order.append(o)
            for u in o.users:
                u.rdy = max(u.rdy, o.end + (0.0 if (u.eng == e and o.dma is None) else HOP))
                u.nin -= 1
                if u.nin == 0:
                    heapq.heappush(ready[u.eng], (u.idx, u))
        self.ops = order
        return max(o.end for o in order)

    def finalize(self, es, final=()):
        nc = self.nc
        import os
        if not os.environ.get("NOSCHED"):
            self.est = self.schedule()
        pos = {id(o): i for i, o in enumerate(self.ops)}
        wdeps = {}
        for o in self.ops:
            rep = {}
            lst = []
            for d in o.deps:
                if d.dma is not None:
                    lst.append(d)
                    continue
                if d.eng == "pe" and o.eng == "pe" and o.dma is None:
                    continue
                r = rep.get(d.eng)
                if r is None or pos[id(d)] > pos[id(r)]:
                    rep[d.eng] = d
            for d in rep.values():
                d.signaled = True
                lst.append(d)
            wdeps[id(o)] = lst
        cnt = {}
        sems = {}

        def sem(key):
            if key not in sems:
                sems[key] = es.enter_context(nc.semaphore("s_" + "_".join(str(x) for x in key)))
            return sems[key]

        for o in self.ops:
            if o.dma is not None:
                key = ("dma", o.dma)
                cnt[key] = cnt.get(key, 0) + 16
                o.tok = (key, cnt[key])
            elif o.signaled:
                key = ("eng", o.eng)
                cnt[key] = cnt.get(key, 0) + 1
                o.tok = (key, cnt[key])
        known = {e: {} for e in self.h}
        nwait = 0
        for o in self.ops:
            E = o.eng
            kn = known[E]
            need = {}
            for d in wdeps[id(o)]:
                s, v = d.tok
                if kn.get(s, 0) >= v:
                    continue
                if need.get(s, 0) < v:
                    need[s] = v
            embed = None
            items = [(s, v) for s, v in need.items() if kn.get(s, 0) < v]
            if items and E in ("act", "dve", "pool") and o.dma is None:
                embed = items.pop()
            for s, v in items:
                self.h[E].wait_ge(sem(s), v)
                nwait += 1
                kn[s] = v
            if embed is not None:
                kn[embed[0]] = embed[1]
            for d in wdeps[id(o)]:
                if d.clock is not None:
                    for s, v in d.clock.items():
                        if kn.get(s, 0) < v:
                            kn[s] = v
            ins = o.fn()
            if embed is not None:
                ins._wait_ge(sem(embed[0]), embed[1])
            if o.dma is not None:
                ins.then_inc(sem(o.tok[0]), 16)
            elif o.signaled:
                ins.then_inc(sem(o.tok[0]), 1)
            if o.tok is not None:
                ck = dict(kn)
                ck[o.tok[0]] = max(ck.get(o.tok[0], 0), o.tok[1])
                o.clock = ck
        fin = {}
        for o in final:
            s, v = o.tok
            fin[s] = max(fin.get(s, 0), v)
        for s, v in fin.items():
            self.h["sp"].wait_ge(sem(s), v)
        return nwait


def _interleave(gens):
    gens = [g for g in gens if g is not None]
    import os
    if os.environ.get("SEQ"):
        for g in gens:
            for _ in g:
                pass
        return
    while gens:
        nxt = []
        for g in gens:
            try:
                next(g)
                nxt.append(g)
            except StopIteration:
                pass
        gens = nxt


def build_program():
    nc = bass.Bass("TRN2", target_bir_lowering=False)
    dt = nc.dram_tensor
    x_d = dt("x", [NT, D], F32, kind="ExternalInput").ap()
    xh_d = dt("xh", [128, D], F32, kind="ExternalInput").ap()
    p_d = dt("p", [NT, DPLE], F32, kind="ExternalInput").ap()
    wA_d = dt("w_inA", [128, 12, 8, 128], F32, kind="ExternalInput").ap()
    wB_d = dt("w_inB", [128, 3, 8, 512], F32, kind="ExternalInput").ap()
    w_out_d = dt("w_out", [128, 8, D], F32, kind="ExternalInput").ap()
    w_gate_d = dt("w_gate", [128, 8, D], F32, kind="ExternalInput").ap()
    w_ple_d = dt("w_ple", [128, 2, D], F32, kind="ExternalInput").ap()
    wsT_d = dt("wsT", [128, 8, 128], F32, kind="ExternalInput").ap()
    cw_d = dt("cw", [128, 4, NTAP], F32, kind="ExternalInput").ap()
    colp_d = dt("colp", [128, NCOL], F32, kind="ExternalInput").ap()
    bc_d = dt("bc", [128, 4096], F32, kind="ExternalInput").ap()
    rows_d = dt("rows", [2, 3072], F32, kind="ExternalInput").ap()
    out_d = dt("out", [NT, D], F32, kind="ExternalOutput").ap()

    es = ExitStack()
    with es:
        def sb(name, shape, dtype):
            return es.enter_context(nc.sbuf_tensor(name, shape, dtype))

        wA = sb("wA", [128, 12, 8, 128], BF16)
        wB = sb("wB", [128, 3, 8, 512], BF16)
        wAf = wA[:].rearrange("p b k c -> p (b k c)")
        wBf = wB[:].rearrange("p b k c -> p (b k c)")
        w_out = wAf[:, 0:8192].rearrange("p (k c) -> p k c", k=8)
        w_ple = wAf[:, 8192:10240].rearrange("p (k c) -> p k c", k=2)
        w_gate = wBf[:, 0:8192].rearrange("p (k c) -> p k c", k=8)
        y_all = sb("y_all", [128, 8, NT], BF16)
        wmT = sb("wmT", [128, 8, 128], BF16)
        dgw = sb("dgw", [128, NTAP, 4, 32], BF16)
        cw = sb("cw_s", [128, 4, NTAP], F32)
        colp = sb("colp_s", [128, NCOL], F32)
        bc = sb("bc_s", [128, 4096], F32)
        rows2 = sb("rows2", [128, 3072], BF16)
        ones2 = sb("ones2", [128, 128], BF16)
        ident_b = sb("ident_b", [128, 128], BF16)
        ident_f = sb("ident_f", [128, 128], F32)
        i32 = sb("i32", [128, 32], F32)
        neghalf = sb("neghalf", [128, 2], F32)
        st1 = sb("st1", [128, NSUB + 1, 4], F32)
        stt_ = sb("stt_", [128, 16, 4], F32)
        bnst = sb("bnst", [128, 8, 2, 6], F32)
        import os
        RS = int(os.environ.get("RS", "2"))
        xb = [sb(f"xb{i}", [128, D], F32) for i in range(RS)]
        xhat = [sb(f"xhat{i}", [128, D], BF16) for i in range(RS)]
        hT = [sb(f"hT{i}", [128, 8, ST], BF16) for i in range(2)]
        a_buf = [sb(f"a_buf{i}", [128, 4, 32 + ST], BF16) for i in range(2)]
        sig = [sb(f"sig{i}", [128, ST], F32) for i in range(2)]
        s_az = sb("s_az", [128, 4, ST], F32)
        s_ln = [sb(f"s_ln{i}", [128, ST], F32) for i in range(2)]
        cbuf = sb("cbuf", [128, 4, ST], F32)
        cn = sb("cn", [128, 4, 512], F32)
        ub = [sb(f"ub{i}", [128, 512], F32) for i in range(2)]
        gvb = [sb(f"gvb{i}", [128, 512], F32) for i in range(2)]
        szb = [sb(f"szb{i}", [128, 512], F32) for i in range(2)]
        vb = [sb(f"vb{i}", [128, 512], BF16) for i in range(2)]
        ybb = [sb(f"ybb{i}", [128, 512], BF16) for i in range(2)]
        pbf = [sb(f"pbf{i}", [128, DPLE], BF16) for i in range(2)]

        class RB:
            def __init__(self, ap, key, legacy):
                self.ap, self.key, self.legacy, self.first = ap, key, list(legacy), True

            def wkeys(self):
                if self.first:
                    self.first = False
                    return [self.key] + self.legacy
                return [self.key]

        def halves(t3):
            return [t3[:, 0:2, :].rearrange("p a b -> p (a b)"), t3[:, 2:4, :].rearrange("p a b -> p (a b)")]
        hT0f = hT[0][:].rearrange("p k c -> p (k c)").bitcast(F32)
        hT1f = hT[1][:].rearrange("p k c -> p (k c)").bitcast(F32)
        hk0 = [("hT", 0, s) for s in range(4)]
        hk1 = [("hT", 1, s) for s in range(4)]
        R = 4
        xr_r = [RB(xb[0][:], ("xb", 0), []), RB(xb[1][:], ("xb", 1), []),
                RB(hT0f[:, 0:1024], ("xr", 2), hk0), RB(hT0f[:, 1024:2048], ("xr", 3), hk0)]
        zb_r = [RB(halves(cbuf)[i], ("zb", i), [("cbuf", 2 * i), ("cbuf", 2 * i + 1)]) for i in range(2)] + \
               [RB(halves(cn)[i], ("zb", 2 + i), [("cn", 2 * i), ("cn", 2 * i + 1)]) for i in range(2)]
        gb_r = [RB(halves(s_az)[i], ("gb", i), [("s_az", 2 * i), ("s_az", 2 * i + 1)]) for i in range(2)] + \
               [RB(hT1f[:, 0:1024], ("gb", 2), hk1), RB(hT1f[:, 1024:2048], ("gb", 3), hk1)]
        h2bf_r = [RB(sig[i][:].bitcast(BF16), ("h2bf", i), [("sig", i)]) for i in range(2)] + \
                 [RB(s_ln[i][:].bitcast(BF16), ("h2bf", 2 + i), [("s_ln", i)]) for i in range(2)]
        h2T_r = [RB(ub[i][:].bitcast(BF16), ("h2T", i), [("ub", i)]) for i in range(2)] + \
                [RB(gvb[i][:].bitcast(BF16), ("h2T", 2 + i), [("gvb", i)]) for i in range(2)]
        pb_r = [RB(szb[i // 2][:, (i % 2) * 256:(i % 2) * 256 + 256], ("pbuf", i), [("szb", i // 2)]) for i in range(4)]
        pT_r = [RB(vb[i][:, 0:256], ("pT", i), [("vb", i)]) for i in range(2)] + \
               [RB(ybb[i][:, 0:256], ("pT", 2 + i), [("ybb", i)]) for i in range(2)]
        TB = es.enter_context(nc.psum_tensor("TB", [128, 2, 1024], BF16))
        G = es.enter_context(nc.psum_tensor("G", [128, 6, 512], F32))

        em = Em(nc)
        pe, act, dve, pool, sp = nc.tensor, nc.scalar, nc.vector, nc.gpsimd, nc.sync

        ag_b = bc[:, 0:1024]
        pg_b = bc[:, 1024:2048]
        pb_b = bc[:, 2048:3072]
        sg_b = bc[:, 3072:3584]
        sb_b = bc[:, 3584:4096]
        ab_rows = rows2[:, 0:1024]
        bg_rows = rows2[:, 1024:2048]
        bs_rows = rows2[:, 2048:3072]
        WA_ALL = [("wA", i) for i in range(12)]
        WB_ALL = [("wB", i) for i in range(3)]
        LA_VAL, LA_GATE, LA_Z = 0, 512, 1024
        LB_U, LB_V, LB_Z = 0, 512, 1024

        rr = {"bn": 0, "stt": 0, "clk": 0}
        g_last = [0] * 6
        tb_last = [0, 0]

        def tick():
            rr["clk"] += 1
            return rr["clk"]

        g_busy = [False] * 6

        def galloc(n=1):
            if n == 2:
                pairs = sorted(((max(g_last[2 * i], g_last[2 * i + 1]), 2 * i) for i in range(3)
                                if not (g_busy[2 * i] or g_busy[2 * i + 1])))
                assert pairs, "no free PSUM bank pair"
                b = pairs[0][1]
                t = tick()
                g_last[b] = g_last[b + 1] = t
                g_busy[b] = g_busy[b + 1] = True
                return b
            order = sorted((b for b in range(6) if not g_busy[b]), key=lambda b: g_last[b])
            assert len(order) >= n, "no free PSUM bank"
            t = tick()
            if n == 1:
                g_last[order[0]] = t
                g_busy[order[0]] = True
                return order[0]
            sel = sorted(order[:n])
            for b in sel:
                g_last[b] = t
                g_busy[b] = True
            return sel

        def gfree(*banks):
            t = tick()
            for b in banks:
                assert g_busy[b]
                g_busy[b] = False
                g_last[b] = t

        def gtouch(*banks):
            pass

        def tballoc():
            t = 0 if tb_last[0] <= tb_last[1] else 1
            tb_last[t] = tick()
            return t

        def load_small():
            em.op("sp", lambda: sp.dma_start(out=colp[:], in_=colp_d), writes=["colp"], dma="colp")
            em.op("sp", lambda: sp.dma_start(out=cw[:], in_=cw_d), writes=["cw"], dma="cw")
        import os
        WORDER = os.environ.get("WORDER", "GAB")
        US_PER_MIB = 4.2
        wstate = {"mib": 0.0}

        def wdma(fn_, wkeys, name, mib, after=()):
            wstate["mib"] += mib
            em.op("pool", fn_, reads=list(after), writes=wkeys, dma=name, c=3.0 + US_PER_MIB * wstate["mib"])

        def w_glu(qs=(0, 1, 2, 3), after=()):
            for q in qs:
                for kb in (q, 4 + q):
                    wdma(lambda kb=kb: pool.dma_start(out=wA[:, kb], in_=wA_d[:, kb]),
                         [("wA", kb)], f"wA{kb}", 0.5, after)

        def w_az(after=()):
            for q in range(4):
                wdma(lambda q=q: pool.dma_start(out=wA[:, 8 + q], in_=wA_d[:, 8 + q]),
                     [("wA", 8 + q)], f"wA{8 + q}", 0.5, after)

        def w_b(after=()):
            for i in (2, 1, 0):
                wdma(lambda i=i: pool.dma_start(out=wB[:, i, 0:4], in_=wB_d[:, i, 0:4]),
                     [("wB", i, 0)], f"wB{i}", 1.0, after)
                wdma(lambda i=i: pool.dma_start(out=wB[:, i, 4:8], in_=wB_d[:, i, 4:8]),
                     [("wB", i)], f"wB{i}", 1.0, after)
        w_glu(qs=(0,))

        def weights_rest():
            w_glu(qs=(1, 2, 3), after=[("st1", 2)])
            w_b(after=[("st1", 3)])
            w_az(after=[("st1", 4)])
            wdma(lambda: pool.dma_start(out=wmT[:], in_=wsT_d), ["wmT"], "wmT", 0.5, [("st1", 4)])
        em.op("pool", lambda: pool.memset(ident_b[:], 0.0), writes=["ident_b"])
        em.op("pool", lambda: pool.affine_select(out=ident_b[:], in_=ident_b[:], compare_op=ALU.not_equal, fill=1.0,
                                                 base=0, pattern=[[-1, 128]], channel_multiplier=1),
              reads=["ident_b"], writes=["ident_b"])
        em.op("pool", lambda: pool.memset(ident_f[:], 0.0), writes=["ident_f"])
        em.op("pool", lambda: pool.affine_select(out=ident_f[:], in_=ident_f[:], compare_op=ALU.not_equal, fill=1.0,
                                                 base=0, pattern=[[-1, 128]], channel_multiplier=1),
              reads=["ident_f"], writes=["ident_f"])
        em.op("pool", lambda: pool.memset(neghalf[:, 0:1], -0.5), writes=["neghalf"])
        em.op("pool", lambda: pool.memset(neghalf[:, 1:2], -1.0), writes=["neghalf"])
        em.op("pool", lambda: pool.memset(ones2[:], 1.0), writes=["ones2"])
        em.op("pool", lambda: pool.memset(rows2[:], 0.0), writes=["rows2"])

        def setup_late():
            LATE = [("hT", 0, 3)]
            em.op("sp", lambda: sp.dma_start(out=bc[:, 3072:4096], in_=bc_d[:, 3072:4096]), reads=[("st1", 2)], writes=["bc_s"], dma="bc_s", c=6.0)
            for i in range(4):
                em.op("dve", lambda i=i: dve.tensor_copy(out=i32[32 * i:32 * i + 32, :],
                                                         in_=ident_f[32 * i:32 * i + 32, 32 * i:32 * i + 32]),
                      reads=LATE + ["ident_f"], writes=["i32"])
            for j in range(4):
                em.op("dve", lambda j=j: dve.scalar_tensor_tensor(
                    out=dgw[:, :, j, :], in0=cw[:, j, :].unsqueeze(2).broadcast_to([128, NTAP, 32]),
                    scalar=0.5, in1=i32[:].unsqueeze(1).broadcast_to([128, NTAP, 32]),
                    op0=ALU.mult, op1=ALU.mult), reads=LATE + ["cw", "i32"], writes=["dgw"])
            em.op("pool", lambda: pool.affine_select(out=wmT[:], in_=wmT[:], compare_op=ALU.is_ge, fill=0.0, base=0,
                                                     pattern=[[0, 8], [1, 128]], channel_multiplier=-1),
                  reads=["wmT"], writes=["wmT"])
            t_r = [cbuf[0:2, 0:2, :].rearrange("p a b -> p (a b)"), cbuf[0:2, 2:4, :].rearrange("p a b -> p (a b)"),
                   cn[0:2, 0:2, :].rearrange("p a b -> p (a b)")]
            t_k = [[("cbuf", 0), ("cbuf", 1)], [("cbuf", 2), ("cbuf", 3)], [("cn", 0), ("cn", 1)]]
            for si in range(3):
                em.op("sp", lambda si=si: sp.dma_start(out=t_r[si], in_=rows_d[:, si * 1024:(si + 1) * 1024]), writes=t_k[si], dma=f"rows{si}")
            em.op("dve", lambda: dve.tensor_scalar(out=t_r[0], in0=t_r[0], scalar1=ALPHA, scalar2=None, op0=ALU.mult),
                  reads=LATE + t_k[0], writes=t_k[0])
            hi = s_ln[0][:].bitcast(BF16)[0:2, :]
            lo = s_ln[1][:].bitcast(BF16)[0:2, :]
            hi32 = cn[0:2, 2:4, :].rearrange("p a b -> p (a b)")
            for si in range(3):
                dst = rows2[0:2, si * 1024:(si + 1) * 1024]
                r32 = t_r[si]
                key = t_k[si]
                em.op("dve", lambda r32=r32: dve.tensor_copy(out=hi, in_=r32), reads=LATE + key, writes=[("s_ln", 0)])
                em.op("dve", lambda: dve.tensor_copy(out=hi32, in_=hi), reads=[("s_ln", 0)], writes=[("cn", 2), ("cn", 3)])
                em.op("dve", lambda r32=r32: dve.tensor_tensor(out=hi32, in0=r32, in1=hi32, op=ALU.subtract),
                      reads=key + [("cn", 2), ("cn", 3)], writes=[("cn", 2), ("cn", 3)])
                em.op("dve", lambda: dve.tensor_scalar(out=lo, in0=hi32, scalar1=colp[0:2, C_M1:C_M1 + 1], scalar2=None, op0=ALU.mult),
                      reads=[("cn", 2), ("cn", 3), "colp"], writes=[("s_ln", 1)])
                em.op("dve", lambda dst=dst: dve.scalar_tensor_tensor(out=dst, in0=hi, scalar=colp[0:2, C_M0:C_M0 + 1], in1=lo,
                                                                      op0=ALU.mult, op1=ALU.add),
                      reads=[("s_ln", 0), ("s_ln", 1), "colp"], writes=["rows2"])

        def ln_stats(src_aps, src_keys, stat_ap, stat_key, need_nmr=True):
            bi = rr["bn"]
            rr["bn"] = (bi + 1) % 8
            n = len(src_aps)
            for j, s in enumerate(src_aps):
                em.op("dve", lambda j=j, s=s: dve.bn_stats(out=bnst[:, bi, j, :], in_=s), reads=src_keys, writes=[("bnst", bi)])
            em.op("dve", lambda: dve.bn_aggr(out=stat_ap[:, 0:2], in_=bnst[:, bi, 0:n, :].rearrange("p a b -> p (a b)")),
                  reads=[("bnst", bi)], writes=[stat_key], n=1)
            em.op("dve", lambda: dve.tensor_scalar(out=stat_ap[:, 3:4], in0=stat_ap[:, 1:2], scalar1=EPS, scalar2=None,
                                                   op0=ALU.add), reads=[stat_key], writes=[stat_key], n=1)
            em.op("pool", lambda: pool.tensor_tensor(out=stat_ap[:, 2:3], in0=stat_ap[:, 3:4], in1=neghalf[:, 0:1], op=ALU.pow),
                  reads=[stat_key, "neghalf"], writes=[stat_key], c=0.5)
            if need_nmr:
                em.op("dve", lambda: dve.scalar_tensor_tensor(out=stat_ap[:, 3:4], in0=stat_ap[:, 0:1], scalar=-1.0,
                                                              in1=stat_ap[:, 2:3], op0=ALU.mult, op1=ALU.mult),
                      reads=[stat_key], writes=[stat_key], n=1)

        def tmp_stat():
            i = rr["stt"]
            rr["stt"] = (i + 1) % 16
            return stt_[:, i, :], ("stt", i)

        def s1_load(src_d, row0, sidx):
            slot = sidx % RS
            xt = xb[slot]
            xk = ("xb", slot)
            em.op("sp", lambda: sp.dma_start(out=xt[:], in_=src_d[row0:row0 + 128, :]), writes=[xk], dma=f"xb{slot}", n=1024)
            sa = st1[:, sidx, :]
            sk = ("st1", sidx)
            ln_stats([xt[:, 0:512], xt[:, 512:1024]], [xk], sa, sk)
            em.op("act", lambda: act.activation(out=xhat[slot][:], in_=xt[:], func=AF.Identity, bias=sa[:, 3:4], scale=sa[:, 2:3]),
                  reads=[xk, sk], writes=[("xhat", slot)], n=1024)

        def s1_trans(sidx, hs, col0):
            slot = sidx % RS
            tb = tballoc()
            for k in range(8):
                em.op("pe", lambda k=k: pe.transpose(out=TB[:, tb, k * 128:(k + 1) * 128], in_=xhat[slot][:, k * 128:(k + 1) * 128],
                                                     identity=ident_b[:]),
                      reads=[("xhat", slot), "ident_b"], writes=[("TB", tb)])
            hk = ("hT", hs, col0 // 128)
            for k in range(8):
                if k % 2 == 0:
                    em.op("act", lambda k=k: act.activation(out=hT[hs][:, k, col0:col0 + 128], in_=TB[:, tb, k * 128:(k + 1) * 128],
                                                            func=AF.Identity, bias=colp[:, C_BE + k:C_BE + k + 1],
                                                            scale=colp[:, C_GE + k:C_GE + k + 1]),
                          reads=[("TB", tb), "colp"], writes=[hk], n=128)
                else:
                    em.op("dve", lambda k=k: dve.tensor_scalar(out=hT[hs][:, k, col0:col0 + 128], in0=TB[:, tb, k * 128:(k + 1) * 128],
                                                               scalar1=colp[:, C_GE + k:C_GE + k + 1], scalar2=colp[:, C_BE + k:C_BE + k + 1],
                                                               op0=ALU.mult, op1=ALU.add),
                          reads=[("TB", tb), "colp"], writes=[hk], n=128)

        def gen_S1(st):
            hs = st % 2
            base = st * 4
            for s in range(min(RS - 1, 4)):
                s1_load(x_d, (base + s) * 128, 1 + base + s)
                yield
            for s in range(4):
                if s + RS - 1 < 4:
                    s1_load(x_d, (base + s + RS - 1) * 128, 1 + base + s + RS - 1)
                    yield
                s1_trans(1 + base + s, hs, s * 128)
                yield

        def proj_fm(bank, loc_off, kb, n, hs, hkeys):
            for k in range(8):
                em.op("pe", lambda k=k: pe.matmul(G[:, bank, 0:n], lhsT=wA[:, kb, k, :], rhs=hT[hs][:, k, 0:n],
                                                  start=(k == 0), stop=(k == 7)),
                      reads=[("wA", kb)] + hkeys, writes=[("G", bank)], n=n)

        def glu_chunk(q, n, hs, hkeys, dst_ap, dkey, src_lo):
            bv = galloc(2)
            bg = bv + 1
            proj_fm(bv, LA_VAL + q * 128, q, n, hs, hkeys)
            proj_fm(bg, LA_GATE + q * 128, 4 + q, n, hs, hkeys)
            sl = q % 2
            em.op("act", lambda: act.activation(out=sig[sl][:, 0:n], in_=G[:, bg, 0:n], func=AF.Tanh, scale=0.5),
                  reads=[("G", bg)], writes=[("sig", sl)], n=n)
            em.op("dve", lambda: dve.scalar_tensor_tensor(out=dst_ap, in0=sig[sl][:, src_lo:n], scalar=1.0, in1=G[:, bv, src_lo:n],
                                                          op0=ALU.add, op1=ALU.mult),
                  reads=[("sig", sl), ("G", bv)], writes=[dkey], n=n - src_lo)
            gfree(bv, bg)

        def az_chunk(q, hs, hkeys):
            bz = galloc(1)
            proj_fm(bz, LA_Z + q * 128, 8 + q, ST, hs, hkeys)
            em.op("act", lambda: act.activation(out=s_az[:, q, :], in_=G[:, bz, :], func=AF.Silu),
                  reads=[("G", bz)], writes=[("s_az", q)], tbl="silu")
            gfree(bz)

        def emit_pass2_wA():
            o_wo = em.op("pool", lambda: pool.dma_start(out=w_out[:, 0:2, :], in_=w_out_d[:, 0:2, :]), writes=WA_ALL + ["w_out"], dma="w_out", c=9.0)
            for kk in (2, 4, 6):
                o_n = em.op("pool", lambda kk=kk: pool.dma_start(out=w_out[:, kk:kk + 2, :], in_=w_out_d[:, kk:kk + 2, :]), writes=["w_out"],
                            dma="w_out", c=9.0 + 2.1 * kk)
                o_n.deps = list(o_wo.deps)
            o_wp = em.op("pool", lambda: pool.dma_start(out=w_ple, in_=w_ple_d), writes=["w_ple"], dma="w_ple", c=30.0)
            o_wp.deps = list(o_wo.deps)

        def gen_Ahead(st):
            hs = st % 2
            asl = st % 2
            hall = [("hT", hs, s) for s in range(4)]
            akeys = [("a_buf", asl, q) for q in range(4)]
            for q in range(4):
                glu_chunk(q, ST, hs, hall, a_buf[asl][:, q, 32:32 + ST], ("a_buf", asl, q), 0)
                yield

        def head_copy(st):
            asl = st % 2
            akeys = [("a_buf", asl, q) for q in range(4)]
            nkeys = [("a_buf", 1 - asl, q) for q in range(4)]
            em.op("pool", lambda: pool.tensor_copy(out=a_buf[1 - asl][:, :, 0:32], in_=a_buf[asl][:, :, ST:ST + 32]),
                  reads=akeys, writes=nkeys)

        def gen_Atail(st):
            asl = st % 2
            akeys = [("a_buf", asl, q) for q in range(4)]
            for half in range(2):
                gpair = galloc(2)
                gc = {2 * half: gpair, 2 * half + 1: gpair + 1}
                for tap in range(NTAP):
                    for i in (2 * half, 2 * half + 1):
                        for j in range(4):
                            em.op("pe", lambda tap=tap, i=i, j=j, gb_=gc[i]: pe.matmul(
                                G[32 * j:32 * j + 32, gb_, :], lhsT=dgw[32 * i:32 * i + 32, tap, j, :],
                                rhs=a_buf[asl][32 * i:32 * i + 32, j, 2 + tap:2 + tap + ST],
                                start=(tap == 0), stop=(tap == NTAP - 1), tile_position=(32 * i, 32 * j)),
                                reads=["dgw"] + akeys, writes=[("G", gc[i])], c=0.034, tbl="c32")
                    if tap % 8 == 7:
                        yield
                for q in (2 * half, 2 * half + 1):
                    em.op("act", lambda q=q, gb_=gc[q]: act.activation(out=cbuf[:, q, :], in_=G[:, gb_, :], func=AF.Identity,
                                                                       bias=colp[:, C_CB + q:C_CB + q + 1], scale=1.0),
                          reads=[("G", gc[q]), "colp"], writes=[("cbuf", q)])
                gfree(gpair, gpair + 1)
                yield

            def fwd(s):
                gt = galloc(1)
                for q in range(4):
                    em.op("pe", lambda q=q: pe.transpose(out=G[:, gt, q * 128:(q + 1) * 128], in_=cbuf[:, q, s * 128:(s + 1) * 128],
                                                         identity=ident_f[:]),
                          reads=[("cbuf", q), "ident_f"], writes=[("G", gt)], c=0.15)
                sa, sk = tmp_stat()
                ln_stats([G[:, gt, :]], [("G", gt)], sa, sk)
                em.op("act", lambda: act.activation(out=cn[:, s, :], in_=G[:, gt, :], func=AF.Identity, bias=sa[:, 3:4], scale=sa[:, 2:3]),
                      reads=[("G", gt), sk], writes=[("cn", s)])
                gfree(gt)
            for s in range(4):
                fwd(s)
                yield
            yield
            yield
            for q in range(4):
                gbk = galloc(1)
                for s in range(4):
                    em.op("pe", lambda s=s, q=q, gbk=gbk: pe.transpose(out=G[:, gbk, s * 128:(s + 1) * 128], in_=cn[:, s, q * 128:(q + 1) * 128],
                                                                       identity=ident_f[:]),
                          reads=[("cn", s), "ident_f"], writes=[("G", gbk)], c=0.15)
                sl = q % 2
                em.op("act", lambda q=q, sl=sl, gbk=gbk: act.activation(out=s_ln[sl][:], in_=G[:, gbk, :], func=AF.Silu,
                                                                        bias=colp[:, C_LB + q:C_LB + q + 1], scale=colp[:, C_LG + q:C_LG + q + 1]),
                      reads=[("G", gbk), "colp"], writes=[("s_ln", sl)], tbl="silu")
                gfree(gbk)
                em.op("dve", lambda q=q, sl=sl: dve.tensor_tensor(out=y_all[:, q, st * ST:(st + 1) * ST], in0=s_ln[sl][:], in1=s_az[:, q, :], op=ALU.mult),
                      reads=[("s_ln", sl), ("s_az", q)], writes=[("ya", st, q)])
                if st + 1 < NST:
                    nh = (st + 1) % 2
                    az_chunk(q, nh, [("hT", nh, s) for s in range(4)])
                yield

        def B_P(st, s):
            hs = st % 2
            c0 = s * 128
            bs_ = s % 2
            hk = [("hT", hs, s)]
            order = (("z", LB_Z, 2), ("v", LB_V, 1), ("u", LB_U, 0)) if s % 2 == 0 else (("v", LB_V, 1), ("u", LB_U, 0), ("z", LB_Z, 2))
            dsts = {"z": (szb[bs_], ("szb", bs_), AF.Silu), "v": (gvb[bs_], ("gvb", bs_), AF.Gelu), "u": (ub[bs_], ("ub", bs_), AF.Gelu)}
            for name, off, kb in order:
                b = galloc(1)
                for k in range(8):
                    em.op("pe", lambda k=k, b=b, kb=kb: pe.matmul(G[:, b, :], lhsT=hT[hs][:, k, c0:c0 + 128], rhs=wB[:, kb, k, :],
                                                                    start=(k == 0), stop=(k == 7)),
                          reads=[("wB", kb)] + hk, writes=[("G", b)])
                dt_, dk_, fn_ = dsts[name]
                em.op("act", lambda b=b, dt_=dt_, fn_=fn_: act.activation(out=dt_[:], in_=G[:, b, :], func=fn_), reads=[("G", b)], writes=[dk_],
                      tbl=("silu" if name == "z" else "gelu"))
                gfree(b)
            sa, sk = tmp_stat()
            ln_stats([gvb[bs_][:]], [("gvb", bs_)], sa, sk)
            em.op("dve", lambda: dve.scalar_tensor_tensor(out=gvb[bs_][:], in0=gvb[bs_][:], scalar=sa[:, 0:1], in1=sg_b, op0=ALU.subtract, op1=ALU.mult),
                  reads=[("gvb", bs_), sk, "bc_s"], writes=[("gvb", bs_)])
            em.op("dve", lambda: dve.scalar_tensor_tensor(out=vb[bs_][:], in0=gvb[bs_][:], scalar=sa[:, 2:3], in1=sb_b, op0=ALU.mult, op1=ALU.add),
                  reads=[("gvb", bs_), sk, "bc_s"], writes=[("vb", bs_)])
            em.op("pool", lambda: pool.tensor_tensor(out=ub[bs_][:], in0=ub[bs_][:], in1=szb[bs_][:], op=ALU.mult),
                  reads=[("ub", bs_), ("szb", bs_)], writes=[("ub", bs_)])

        def B_Q(st, s):
            bs_ = s % 2
            gs = st * 4 + s
            bsg = galloc(1)
            for h in range(8):
                em.op("pe", lambda h=h: pe.matmul(G[:, bsg, h * 64:(h + 1) * 64], lhsT=wmT[:, h, :], rhs=vb[bs_][:, h * 64:(h + 1) * 64],
                                                  start=True, stop=False), reads=["wmT", ("vb", bs_)], writes=[("G", bsg)], c=0.03)
                em.op("pe", lambda h=h: pe.matmul(G[:, bsg, h * 64:(h + 1) * 64], lhsT=bs_rows[:, h * 128:(h + 1) * 128], rhs=ones2[:, 0:64],
                                                  start=False, stop=True), reads=["rows2", "ones2"], writes=[("G", bsg)], c=0.03)
            em.op("dve", lambda: dve.tensor_tensor(out=ybb[bs_][:], in0=G[:, bsg, :], in1=ub[bs_][:], op=ALU.mult),
                  reads=[("G", bsg), ("ub", bs_)], writes=[("ybb", bs_)])
            gfree(bsg)

        def B_Q2(st, s):
            bs_ = s % 2
            gs = st * 4 + s
            tb = tballoc()
            for e in range(4):
                em.op("pe", lambda e=e: pe.transpose(out=TB[:, tb, e * 128:(e + 1) * 128], in_=ybb[bs_][:, e * 128:(e + 1) * 128], identity=ident_b[:]),
                      reads=[("ybb", bs_), "ident_b"], writes=[("TB", tb)])
            em.op("act", lambda: act.copy(out=y_all[:, 4:8, gs * 128:(gs + 1) * 128], in_=TB[:, tb, 0:512].rearrange("p (e t) -> p e t", e=4)),
                  reads=[("TB", tb)], writes=[("yb", gs)])

        def gen_B(st):
            B_P(st, 0)
            yield
            B_P(st, 1)
            yield
            B_Q(st, 0)
            yield
            B_P(st, 2)
            yield
            B_Q2(st, 0)
            B_Q(st, 1)
            yield
            B_P(st, 3)
            yield
            B_Q2(st, 1)
            B_Q(st, 2)
            yield
            yield
            B_Q2(st, 2)
            B_Q(st, 3)
            yield
            yield
            B_Q2(st, 3)
            yield
            if st == NST - 1:
                o_wg = em.op("pool", lambda: pool.dma_start(out=w_gate[:, 0:2, :], in_=w_gate_d[:, 0:2, :]), writes=WB_ALL + ["w_gate"],
                             dma="w_gate", c=9.0)
                for kk in (2, 4, 6):
                    o_n = em.op("pool", lambda kk=kk: pool.dma_start(out=w_gate[:, kk:kk + 2, :], in_=w_gate_d[:, kk:kk + 2, :]), writes=["w_gate"],
                                dma="w_gate", c=9.0 + 2.1 * kk)
                    o_n.deps = list(o_wg.deps)

        P2 = {}
        P2V = int(os.environ.get("P2V", "0"))

        def p2_ctx(gs):
            if gs not in P2:
                sl = gs % R
                P2[gs] = dict(sl=sl, r0=gs * 128, xr=xr_r[sl], zb=zb_r[sl], pbu=pb_r[sl], pTt=pT_r[sl], h2b=h2bf_r[sl],
                              h2t=h2T_r[sl], gbf=gb_r[sl], pf=pbf[gs % 2], pfk=("pbf", gs % 2),
                              sa1=st1[:, gs + 1, :], sk1=("st1", gs + 1))
            return P2[gs]

        def p2_loads(gs):
            c = p2_ctx(gs)
            em.op("sp", lambda: sp.dma_start(out=c["xr"].ap, in_=x_d[c["r0"]:c["r0"] + 128, :]), writes=c["xr"].wkeys(), dma=f"xr{c['sl']}", n=1024)
            em.op("sp", lambda: sp.dma_start(out=c["pbu"].ap, in_=p_d[c["r0"]:c["r0"] + 128, :]), writes=c["pbu"].wkeys(), dma=f"pbuf{c['sl']}", n=256)

        def p2_ppath(gs):
            c = p2_ctx(gs)
            pf, pfk, pbu, pTt = c["pf"], c["pfk"], c["pbu"], c["pTt"]
            em.op("act", lambda: act.copy(out=pf[:], in_=pbu.ap), reads=[pbu.key], writes=[pfk], n=256)
            tb2 = tballoc()
            for j in range(2):
                em.op("pe", lambda j=j: pe.transpose(out=TB[:, tb2, j * 128:(j + 1) * 128], in_=pf[:, j * 128:(j + 1) * 128], identity=ident_b[:]),
                      reads=[pfk, "ident_b"], writes=[("TB", tb2)])
            em.op("act", lambda: act.copy(out=pTt.ap, in_=TB[:, tb2, 0:256]), reads=[("TB", tb2)], writes=pTt.wkeys(), n=256)

        def p2_h2T(gs):
            c = p2_ctx(gs)
            h2b, h2t = c["h2b"], c["h2t"]
            tb = tballoc()
            for k in range(8):
                em.op("pe", lambda k=k: pe.transpose(out=TB[:, tb, k * 128:(k + 1) * 128], in_=h2b.ap[:, k * 128:(k + 1) * 128], identity=ident_b[:]),
                      reads=[h2b.key, "ident_b"], writes=[("TB", tb)])
            em.op("act", lambda: act.copy(out=h2t.ap, in_=TB[:, tb, :]), reads=[("TB", tb)], writes=h2t.wkeys(), n=1024)

        def p2_xprep(gs):
            c = p2_ctx(gs)
            xr, sa1, sk1 = c["xr"], c["sa1"], c["sk1"]
            if P2V >= 1:
                em.op("act", lambda: act.activation(out=xr.ap, in_=xr.ap, func=AF.Identity, bias=sa1[:, 3:4], scale=sa1[:, 2:3]),
                      reads=[xr.key, sk1], writes=[xr.key], n=1024)
                em.op("pool", lambda: pool.tensor_tensor(out=xr.ap, in0=xr.ap, in1=ag_b, op=ALU.mult), reads=[xr.key, "bc_l"], writes=[xr.key], n=1024)
                return
            em.op("dve", lambda: dve.scalar_tensor_tensor(out=xr.ap, in0=xr.ap, scalar=sa1[:, 0:1], in1=ag_b, op0=ALU.subtract, op1=ALU.mult),
                  reads=[xr.key, sk1, "bc_l"], writes=[xr.key], n=1024)

        def p2_mix(gs):
            c = p2_ctx(gs)
            r0 = c["r0"]
            st = gs // 4
            yk = [("ya", st, q) for q in range(4)] + [("yb", gs)]
            gm = galloc(2)
            c["gm"] = gm
            for nb in range(2):
                for e in range(8):
                    em.op("pe", lambda nb=nb, e=e: pe.matmul(G[:, gm + nb, :], lhsT=y_all[:, e, r0:r0 + 128], rhs=w_out[:, e, nb * 512:(nb + 1) * 512],
                                                             start=(e == 0), stop=False),
                          reads=["w_out"] + yk, writes=[("G", gm + nb)])
                em.op("pe", lambda nb=nb: pe.matmul(G[:, gm + nb, :], lhsT=ones2[:], rhs=ab_rows[:, nb * 512:(nb + 1) * 512],
                                                    start=False, stop=True), reads=["ones2", "rows2"], writes=[("G", gm + nb)])

        def p2_ln(gs):
            c = p2_ctx(gs)
            xr, zb, h2b, sa1, sk1, gm = c["xr"], c["zb"], c["h2b"], c["sa1"], c["sk1"], c["gm"]
            if P2V >= 1:
                em.op("dve", lambda: dve.tensor_tensor(out=zb.ap.rearrange("p (a b) -> p a b", a=2), in0=xr.ap.rearrange("p (a b) -> p a b", a=2),
                                                       in1=G[:, gm:gm + 2, :], op=ALU.add),
                      reads=[xr.key, ("G", gm), ("G", gm + 1)], writes=zb.wkeys(), n=1024)
            else:
                em.op("dve", lambda: dve.scalar_tensor_tensor(out=zb.ap.rearrange("p (a b) -> p a b", a=2), in0=xr.ap.rearrange("p (a b) -> p a b", a=2),
                                                              scalar=sa1[:, 2:3], in1=G[:, gm:gm + 2, :], op0=ALU.mult, op1=ALU.add),
                      reads=[xr.key, sk1, ("G", gm), ("G", gm + 1)], writes=zb.wkeys(), n=1024)
            gfree(gm, gm + 1)
            sa, sk = tmp_stat()
            ln_stats([zb.ap[:, 0:512], zb.ap[:, 512:1024]], [zb.key], sa, sk, need_nmr=(P2V >= 2))
            if P2V >= 2:
                em.op("act", lambda: act.activation(out=zb.ap, in_=zb.ap, func=AF.Identity, bias=sa[:, 3:4], scale=sa[:, 2:3]),
                      reads=[zb.key, sk], writes=[zb.key], n=1024)
                em.op("pool", lambda: pool.tensor_tensor(out=zb.ap, in0=zb.ap, in1=pg_b, op=ALU.mult), reads=[zb.key, "bc_l"], writes=[zb.key], n=1024)
                em.op("dve", lambda: dve.tensor_tensor(out=zb.ap, in0=zb.ap, in1=pb_b, op=ALU.add), reads=[zb.key, "bc_l"], writes=[zb.key], n=1024)
            else:
                em.op("dve", lambda: dve.scalar_tensor_tensor(out=zb.ap, in0=zb.ap, scalar=sa[:, 0:1], in1=pg_b, op0=ALU.subtract, op1=ALU.mult),
                      reads=[zb.key, sk, "bc_l"], writes=[zb.key], n=1024)
                em.op("dve", lambda: dve.scalar_tensor_tensor(out=zb.ap, in0=zb.ap, scalar=sa[:, 2:3], in1=pb_b, op0=ALU.mult, op1=ALU.add),
                      reads=[zb.key, sk, "bc_l"], writes=[zb.key], n=1024)
            em.op("act", lambda: act.copy(out=h2b.ap, in_=zb.ap), reads=[zb.key], writes=h2b.wkeys(), n=1024)

        def p2_gate(gs):
            c = p2_ctx(gs)
            pTt, h2t, gbf = c["pTt"], c["h2t"], c["gbf"]
            gp = galloc(2)
            for nb in range(2):
                for j in range(2):
                    em.op("pe", lambda nb=nb, j=j: pe.matmul(G[:, gp + nb, :], lhsT=pTt.ap[:, j * 128:(j + 1) * 128], rhs=w_ple[:, j, nb * 512:(nb + 1) * 512],
                                                             start=(j == 0), stop=(j == 1)),
                          reads=["w_ple", pTt.key], writes=[("G", gp + nb)])
            gg = galloc(2)
            for nb in range(2):
                for k in range(8):
                    em.op("pe", lambda nb=nb, k=k: pe.matmul(G[:, gg + nb, :], lhsT=h2t.ap[:, k * 128:(k + 1) * 128], rhs=w_gate[:, k, nb * 512:(nb + 1) * 512],
                                                             start=(k == 0), stop=False),
                          reads=["w_gate", h2t.key], writes=[("G", gg + nb)])
                em.op("pe", lambda nb=nb: pe.matmul(G[:, gg + nb, :], lhsT=ones2[:], rhs=bg_rows[:, nb * 512:(nb + 1) * 512],
                                                    start=False, stop=True), reads=["ones2", "rows2"], writes=[("G", gg + nb)])
            g3 = gbf.ap.rearrange("p (a b) -> p a b", a=2)
            em.op("act", lambda: act.activation(out=g3, in_=G[:, gg:gg + 2, :], func=AF.Tanh, scale=0.5),
                  reads=[("G", gg), ("G", gg + 1)], writes=gbf.wkeys(), n=1024)
            gfree(gg, gg + 1)
            em.op("dve", lambda: dve.scalar_tensor_tensor(out=g3, in0=g3, scalar=1.0, in1=G[:, gp:gp + 2, :], op0=ALU.add, op1=ALU.mult),
                  reads=[gbf.key, ("G", gp), ("G", gp + 1)], writes=[gbf.key], n=1024)
            gfree(gp, gp + 1)

        def p2_out(gs):
            c = p2_ctx(gs)
            gbf, zb, r0 = c["gbf"], c["zb"], c["r0"]
            em.op("pool", lambda: pool.tensor_tensor(out=gbf.ap, in0=gbf.ap, in1=zb.ap, op=ALU.add), reads=[gbf.key, zb.key], writes=[gbf.key], n=1024)
            return em.op("sp", lambda: sp.dma_start(out=out_d[r0:r0 + 128, :], in_=gbf.ap), reads=[gbf.key], writes=[gbf.key], dma=f"out{c['sl']}", n=1024)

        load_small()
        _interleave([gen_S1(0)])
        weights_rest()
        s1_load(xh_d, 0, 0)
        s1_trans(0, 1, 0)
        setup_late()
        for q in range(4):
            glu_chunk(q, 128, 1, [("hT", 1, 0)], a_buf[0][:, q, 0:32], ("a_buf", 0, q), 96)
        em.op("dve", lambda: dve.tensor_scalar(out=a_buf[0][:, :, 0:32], in0=a_buf[0][:, :, 0:32], scalar1=colp[:, C_HM:C_HM + 1], scalar2=None,
                                               op0=ALU.mult),
              reads=[("a_buf", 0, q) for q in range(4)] + ["colp"], writes=[("a_buf", 0, q) for q in range(4)])
        for q in range(4):
            az_chunk(q, 0, [("hT", 0, s) for s in range(4)])
        for st in range(NST):
            _interleave([gen_Ahead(st), gen_B(st), gen_Atail(st - 1) if st >= 1 else None,
                         gen_S1(st + 1) if st + 1 < NST else None])
            if st + 1 < NST:
                head_copy(st)
            if st == 1:
                em.op("sp", lambda: sp.dma_start(out=bc[:, 0:3072], in_=bc_d[:, 0:3072]), reads=[("ya", 0, 0)], writes=["bc_l"], dma="bc_l", c=10.0)
                em.op("dve", lambda: dve.tensor_scalar(out=ag_b, in0=ag_b, scalar1=ALPHA, scalar2=None, op0=ALU.mult),
                      reads=["bc_l"], writes=["bc_l"], n=1024)
        emit_pass2_wA()
        _interleave([gen_Atail(NST - 1)])
        last_out = []
        em.op("dve", lambda: dve.tensor_scalar(out=w_ple, in0=w_ple, scalar1=0.5, scalar2=None, op0=ALU.mult),
              reads=["w_ple"], writes=["w_ple"])
        ok = lambda i: 0 <= i < NSUB
        p2_loads(0)
        for t in range(NSUB + 3):
            if ok(t - 3):
                last_out.append(p2_out(t - 3))
            if ok(t + 1):
                p2_loads(t + 1)
            if ok(t):
                p2_ppath(t)
            if ok(t - 2):
                p2_h2T(t - 2)
            if ok(t):
                p2_xprep(t)
            if ok(t - 1):
                p2_ln(t - 1)
            if ok(t):
                p2_mix(t)
            if ok(t - 2):
                p2_gate(t - 2)
        nwait = em.finalize(es, final=last_out)
        import os
        if os.environ.get("KVERB"):
            print("sched estimate us:", getattr(em, "est", None), "waits:", nwait, "ops:", len(em.ops))
    return nc


def _perm():
    perm = np.zeros(512, dtype=np.int64)
    for qo in range(4):
        for j in range(4):
            for r in range(32):
                perm[qo * 128 + 32 * j + r] = j * 128 + 32 * qo + r
    return perm


_NC_CACHE = {}


def kernel(x, p, ln_emb_g, ln_emb_b, w_in, conv_w, conv_b, conv_ln_g, conv_ln_b, sgu_ln_g, sgu_ln_b, w_s, b_s,
           w_out, post_ln_g, post_ln_b, w_ple, w_ple_gate, b_ple_gate):
    f = lambda a: np.ascontiguousarray(np.asarray(a, dtype=np.float32))
    x, p = f(x), f(p)
    perm = _perm()
    w_in0 = f(w_in)[0]
    colsA = np.concatenate([np.arange(0, 1024), A_Z + perm])
    colsB = np.arange(1536, 3072)
    w_inA_l = np.ascontiguousarray(w_in0[:, colsA].reshape(8, 128, 12, 128).transpose(1, 2, 0, 3))
    w_inB_l = np.ascontiguousarray(w_in0[:, colsB].reshape(8, 128, 3, 512).transpose(1, 2, 0, 3))
    w_out0 = f(w_out)[0]
    rowsel = np.arange(1024)
    rowsel[0:512] = perm
    w_out_l = np.ascontiguousarray(w_out0[rowsel, :].reshape(8, 128, 1024).transpose(1, 0, 2))
    w_gate_l = np.ascontiguousarray(f(w_ple_gate)[0].reshape(8, 128, 1024).transpose(1, 0, 2))
    w_ple_l = np.ascontiguousarray(f(w_ple)[0].reshape(2, 128, 1024).transpose(1, 0, 2))
    wsT = np.ascontiguousarray(f(w_s)[0].transpose(2, 0, 1))
    cw = np.ascontiguousarray(f(conv_w)[0].T.reshape(4, 128, NTAP).transpose(1, 0, 2))
    colp = np.zeros((128, NCOL), np.float32)
    colp[:, C_GE:C_GE + 8] = f(ln_emb_g).reshape(8, 128).T
    colp[:, C_BE:C_BE + 8] = f(ln_emb_b).reshape(8, 128).T
    colp[:, C_CB:C_CB + 4] = f(conv_b)[0][perm].reshape(4, 128).T
    colp[:, C_LG:C_LG + 4] = f(conv_ln_g)[0][perm].reshape(4, 128).T
    colp[:, C_LB:C_LB + 4] = f(conv_ln_b)[0][perm].reshape(4, 128).T
    colp[0, C_M0] = 1.0
    colp[1, C_M1] = 1.0
    bcrow = np.concatenate([f(ln_emb_g), f(post_ln_g)[0], f(post_ln_b)[0], f(sgu_ln_g)[0], f(sgu_ln_b)[0]])
    bc = np.ascontiguousarray(np.broadcast_to(bcrow[None, :], (128, 4096)))
    rrow = np.concatenate([f(ln_emb_b), f(b_ple_gate)[0], f(b_s)[0].reshape(-1)])
    rows = np.ascontiguousarray(np.broadcast_to(rrow[None, :], (2, 3072)))

    B, S, _ = x.shape
    per_b = S // NT
    in_maps = []
    for c in range(NCORES):
        b, j = divmod(c, per_b)
        t0 = j * NT
        xc = x[b, t0:t0 + NT]
        pc = p[0, b, t0:t0 + NT]
        cp = colp.copy()
        if j == 0:
            xh = np.zeros((128, D), np.float32)
            cp[:, C_HM] = 0.0
        else:
            xh = x[b, t0 - 128:t0]
            cp[:, C_HM] = 1.0
        in_maps.append({"x": np.ascontiguousarray(xc), "xh": np.ascontiguousarray(xh), "p": np.ascontiguousarray(pc),
                        "w_inA": w_inA_l, "w_inB": w_inB_l, "w_out": w_out_l, "w_gate": w_gate_l, "w_ple": w_ple_l, "wsT": wsT, "cw": cw,
                        "colp": cp, "bc": bc, "rows": rows})
    nc = build_program()
    res = run_bass_kernel_spmd(nc, in_maps, core_ids=list(range(NCORES)))
    out = np.empty((B, S, D), np.float32)
    for c in range(NCORES):
        b, j = divmod(c, per_b)
        out[b, j * NT:(j + 1) * NT] = res.results[c]["out"]
    return out
```

```python
import numpy as np
from contextlib import ExitStack
import concourse.bass as bass
import concourse.mybir as mybir
from concourse.bass_utils import run_bass_kernel_spmd

F32 = mybir.dt.float32
BF16 = mybir.dt.bfloat16
AF = mybir.ActivationFunctionType
ALU = mybir.AluOpType

NCORES = 8
D = 1024
NT = 2048
NSUB = NT // 128
ST = 512
NST = NT // ST
DPLE = 256
ALPHA = float(2.0 ** 0.25)
EPS = 1e-5
A_VAL, A_GATE, A_Z, B_U, B_V, B_Z = 0, 512, 1024, 1536, 2048, 2560
NTAP = 31
C_GE, C_BE, C_CB, C_LG, C_LB, C_HM, C_M0, C_M1, NCOL = 0, 8, 16, 20, 24, 28, 29, 30, 32


class Op:
    __slots__ = ("eng", "fn", "dma", "deps", "signaled", "tok", "clock", "c", "tbl", "idx", "end", "nin", "users", "rdy")

    def __init__(self, eng, fn, dma):
        self.eng, self.fn, self.dma = eng, fn, dma
        self.deps = ()
        self.signaled = False
        self.tok = None
        self.clock = None


DEF_COST = {"pe": 0.27, "act": 0.8, "dve": 0.8, "pool": 1.5, "sp": 0.4}


class Em:
    def __init__(self, nc):
        self.nc = nc
        self.ops = []
        self.last_w = {}
        self.readers = {}
        self.h = {"pe": nc.tensor, "act": nc.scalar, "dve": nc.vector, "pool": nc.gpsimd, "sp": nc.sync}

    @staticmethod
    def cost(eng, fn, n, dma):
        names = fn.__code__.co_names
        if dma is not None:
            return 2.0 + n * 128 * 4 / 150e3
        if eng == "pe":
            return 0.1 if "transpose" in names else 0.02 + n * 0.00042
        if eng == "act":
            return 0.35 + n / 1400.0
        if eng == "dve":
            if "bn_stats" in names:
                return 0.06 + n / 800.0
            if "bn_aggr" in names:
                return 0.2
            return 0.1 + n / 870.0
        if eng == "pool":
            if "tensor_tensor" in names:
                return 0.3 + n * 0.002
            return 0.5
        return 0.4

    def op(self, eng, fn, reads=(), writes=(), dma=None, c=None, tbl=None, n=512):
        o = Op(eng, fn, dma)
        o.c = self.cost(eng, fn, n, dma) if c is None else c
        if eng == "pe" and tbl is None:
            tbl = "full"
        if eng == "act" and tbl is None:
            names = fn.__code__.co_consts
        o.tbl = tbl
        deps = []
        seen = set()

        def add(d):
            if d is not None and id(d) not in seen:
                seen.add(id(d))
                deps.append(d)

        for k in reads:
            add(self.last_w.get(k))
        for k in writes:
            add(self.last_w.get(k))
            for r in self.readers.get(k, ()):
                add(r)
        o.deps = deps
        for k in reads:
            self.readers.setdefault(k, []).append(o)
        for k in writes:
            self.last_w[k] = o
            self.readers[k] = []
        self.ops.append(o)
        return o

    def schedule(self):
        import heapq
        ops = self.ops
        for i, o in enumerate(ops):
            o.idx = i
            o.users = []
            o.nin = 0
            o.rdy = 0.0
            o.end = None
        for o in ops:
            for d in o.deps:
                d.users.append(o)
                o.nin += 1
        import os
        POL = os.environ.get("SCHED_POL", "fifo")
        SLACK = float(os.environ.get("SCHED_SLACK", "0.3"))
        bl = [0.0] * len(ops)
        for o in reversed(ops):
            m = 0.0
            for u in o.users:
                if bl[u.idx] > m:
                    m = bl[u.idx]
            bl[o.idx] = o.c + m
        ready = {e: [] for e in self.h}
        for o in ops:
            if o.nin == 0:
                heapq.heappush(ready[o.eng], (o.idx, o))
        free_at = {e: 0.0 for e in self.h}
        cur_tbl = [None]
        cur_mode = [None]
        HOP = float(os.environ.get("SCHED_HOP", "0.15"))
        SCALE = {"pe": float(os.environ.get("SCHED_PE", "1.1")), "act": float(os.environ.get("SCHED_ACT", "1.0")),
                 "dve": float(os.environ.get("SCHED_DVE", "1.15")), "pool": float(os.environ.get("SCHED_POOL", "1.0")), "sp": 1.0}
        order = []
        n = len(ops)
        LOOK = int(os.environ.get("SCHED_LOOK", "20"))
        while len(order) < n:
            best = None
            for e, hp in ready.items():
                if not hp:
                    continue
                cands = heapq.nsmallest(LOOK, hp)
                for idx, o in cands:
                    st = max(free_at[e], o.rdy)
                    pen = 0.0
                    if e == "act" and o.tbl is not None and o.tbl != cur_tbl[0]:
                        pen = 1.3
                    if e == "pe" and o.tbl is not None and o.tbl != cur_mode[0]:
                        pen = 0.25
                    if POL == "bl":
                        key = (round((st + pen) / SLACK), 1 if (e == "pool" and o.dma is not None) else 0, -bl[idx], idx)
                    else:
                        key = (st + pen, 1 if (e == "pool" and o.dma is not None) else 0, idx)
                    if best is None or key < best[0]:
                        best = (key, o, st + pen)
            _, o, st = best
            e = o.eng
            ready[e] = [(i, x) for (i, x) in ready[e] if x is not o]
            heapq.heapify(ready[e])
            if o.dma is not None:
                free_at[e] = st + (1.1 if e == "pool" else 0.45)
                o.end = st + o.c
            else:
                free_at[e] = st + o.c * SCALE[e]
                o.end = free_at[e]
            if e == "act" and o.tbl is not None:
                cur_tbl[0] = o.tbl
            if e == "pe" and o.tbl is not None:
                cur_mode[0] = o.tbl
            order.append(o)
            for u in o.users:
                u.rdy = max(u.rdy, o.end + (0.0 if (u.eng == e and o.dma is None) else HOP))
                u.nin -= 1
                if u.nin == 0:
                    heapq.heappush(ready[u.eng], (u.idx, u))
        self.ops = order
        return max(o.end for o in order)

    def finalize(self, es, final=()):
        nc = self.nc
        import os
        if not os.environ.get("NOSCHED"):
            self.est = self.schedule()
        pos = {id(o): i for i, o in enumerate(self.ops)}
        wdeps = {}
        for o in self.ops:
            rep = {}
            lst = []
            for d in o.deps:
                if d.dma is not None:
                    lst.append(d)
                    continue
                if d.eng == "pe" and o.eng == "pe" and o.dma is None:
                    continue
                r = rep.get(d.eng)
                if r is None or pos[id(d)] > pos[id(r)]:
                    rep[d.eng] = d
            for d in rep.values():
                d.signaled = True
                lst.append(d)
            wdeps[id(o)] = lst
        cnt = {}
        sems = {}

        def sem(key):
            if key not in sems:
                sems[key] = es.enter_context(nc.semaphore("s_" + "_".join(str(x) for x in key)))
            return sems[key]

        for o in self.ops:
            if o.dma is not None:
                key = ("dma", o.dma)
                cnt[key] = cnt.get(key, 0) + 16
                o.tok = (key, cnt[key])
            elif o.signaled:
                key = ("eng", o.eng)
                cnt[key] = cnt.get(key, 0) + 1
                o.tok = (key, cnt[key])
        known = {e: {} for e in self.h}
        nwait = 0
        EMBED_ENGS = ("act", "dve", "pool") if os.environ.get("NO_EMBED_PE") else ("act", "dve", "pool", "pe")
        for o in self.ops:
            E = o.eng
            kn = known[E]
            need = {}
            for d in wdeps[id(o)]:
                s, v = d.tok
                if kn.get(s, 0) >= v:
                    continue
                if need.get(s, 0) < v:
                    need[s] = v
            embed = None
            items = [(s, v) for s, v in need.items() if kn.get(s, 0) < v]
            if items and E in EMBED_ENGS and o.dma is None:
                embed = items.pop()
            for s, v in items:
                self.h[E].wait_ge(sem(s), v)
                nwait += 1
                kn[s] = v
            if embed is not None:
                kn[embed[0]] = embed[1]
            for d in wdeps[id(o)]:
                if d.clock is not None:
                    for s, v in d.clock.items():
                        if kn.get(s, 0) < v:
                            kn[s] = v
            ins = o.fn()
            if embed is not None:
                ins._wait_ge(sem(embed[0]), embed[1])
            if o.dma is not None:
                ins.then_inc(sem(o.tok[0]), 16)
            elif o.signaled:
                ins.then_inc(sem(o.tok[0]), 1)
            if o.tok is not None:
                ck = dict(kn)
                ck[o.tok[0]] = max(ck.get(o.tok[0], 0), o.tok[1])
                o.clock = ck
        fin = {}
        for o in final:
            s, v = o.tok
            fin[s] = max(fin.get(s, 0), v)
        for s, v in fin.items():
            self.h["sp"].wait_ge(sem(s), v)
        return nwait


def _interleave(gens):
    gens = [g for g in gens if g is not None]
    import os
    if os.environ.get("SEQ"):
        for g in gens:
            for _ in g:
                pass
        return
    while gens:
        nxt = []
        for g in gens:
            try:
                next(g)
                nxt.append(g)
            except StopIteration:
                pass
        gens = nxt


def build_program():
    nc = bass.Bass("TRN2", target_bir_lowering=False)
    dt = nc.dram_tensor
    x_d = dt("x", [NT, D], F32, kind="ExternalInput").ap()
    xh_d = dt("xh", [128, D], F32, kind="ExternalInput").ap()
    p_d = dt("p", [NT, DPLE], F32, kind="ExternalInput").ap()
    wA_d = dt("w_inA", [128, 12, 8, 128], F32, kind="ExternalInput").ap()
    wB_d = dt("w_inB", [128, 3, 8, 512], F32, kind="ExternalInput").ap()
    w_out_d = dt("w_out", [128, 8, D], F32, kind="ExternalInput").ap()
    w_gate_d = dt("w_gate", [128, 8, D], F32, kind="ExternalInput").ap()
    w_ple_d = dt("w_ple", [128, 2, D], F32, kind="ExternalInput").ap()
    wsT_d = dt("wsT", [128, 8, 128], F32, kind="ExternalInput").ap()
    cw_d = dt("cw", [128, 4, NTAP], F32, kind="ExternalInput").ap()
    colp_d = dt("colp", [128, NCOL], F32, kind="ExternalInput").ap()
    bc_d = dt("bc", [128, 4096], F32, kind="ExternalInput").ap()
    rows_d = dt("rows", [2, 3072], F32, kind="ExternalInput").ap()
    out_d = dt("out", [NT, D], F32, kind="ExternalOutput").ap()

    es = ExitStack()
    with es:
        def sb(name, shape, dtype):
            return es.enter_context(nc.sbuf_tensor(name, shape, dtype))

        wA = sb("wA", [128, 12, 8, 128], BF16)
        wB = sb("wB", [128, 3, 8, 512], BF16)
        wAf = wA[:].rearrange("p b k c -> p (b k c)")
        wBf = wB[:].rearrange("p b k c -> p (b k c)")
        w_out = wAf[:, 0:8192].rearrange("p (k c) -> p k c", k=8)
        w_ple = wAf[:, 8192:10240].rearrange("p (k c) -> p k c", k=2)
        w_gate = wBf[:, 0:8192].rearrange("p (k c) -> p k c", k=8)
        y_all = sb("y_all", [128, 8, NT], BF16)
        wmT = sb("wmT", [128, 8, 128], BF16)
        dgw = sb("dgw", [128, NTAP, 4, 32], BF16)
        cw = sb("cw_s", [128, 4, NTAP], F32)
        colp = sb("colp_s", [128, NCOL], F32)
        bc = sb("bc_s", [128, 4096], F32)
        rows2 = sb("rows2", [128, 3072], BF16)
        ones2 = sb("ones2", [128, 128], BF16)
        ident_b = sb("ident_b", [128, 128], BF16)
        ident_f = sb("ident_f", [128, 128], F32)
        i32 = sb("i32", [128, 32], F32)
        neghalf = sb("neghalf", [128, 2], F32)
        st1 = sb("st1", [128, NSUB + 1, 4], F32)
        stt_ = sb("stt_", [128, 16, 4], F32)
        bnst = sb("bnst", [128, 8, 2, 6], F32)
        import os
        RS = int(os.environ.get("RS", "2"))
        xb = [sb(f"xb{i}", [128, D], F32) for i in range(RS)]
        xhat = [sb(f"xhat{i}", [128, D], BF16) for i in range(RS)]
        hT = [sb(f"hT{i}", [128, 8, ST], BF16) for i in range(2)]
        a_buf = [sb(f"a_buf{i}", [128, 4, 32 + ST], BF16) for i in range(2)]
        sig = [sb(f"sig{i}", [128, ST], F32) for i in range(2)]
        s_az = sb("s_az", [128, 4, ST], F32)
        s_ln = [sb(f"s_ln{i}", [128, ST], F32) for i in range(2)]
        cbuf = sb("cbuf", [128, 4, ST], F32)
        cn = sb("cn", [128, 4, 512], F32)
        ub = [sb(f"ub{i}", [128, 512], F32) for i in range(2)]
        gvb = [sb(f"gvb{i}", [128, 512], F32) for i in range(2)]
        szb = [sb(f"szb{i}", [128, 512], F32) for i in range(2)]
        vb = [sb(f"vb{i}", [128, 512], BF16) for i in range(2)]
        ybb = [sb(f"ybb{i}", [128, 512], BF16) for i in range(2)]
        pbf = [sb(f"pbf{i}", [128, DPLE], BF16) for i in range(2)]

        class RB:
            def __init__(self, ap, key, legacy):
                self.ap, self.key, self.legacy, self.first = ap, key, list(legacy), True

            def wkeys(self):
                if self.first:
                    self.first = False
                    return [self.key] + self.legacy
                return [self.key]

        def halves(t3):
            return [t3[:, 0:2, :].rearrange("p a b -> p (a b)"), t3[:, 2:4, :].rearrange("p a b -> p (a b)")]
        hT0f = hT[0][:].rearrange("p k c -> p (k c)").bitcast(F32)
        hT1f = hT[1][:].rearrange("p k c -> p (k c)").bitcast(F32)
        hk0 = [("hT", 0, s) for s in range(4)]
        hk1 = [("hT", 1, s) for s in range(4)]
        R = 4
        xr_r = [RB(xb[0][:], ("xb", 0), []), RB(xb[1][:], ("xb", 1), []),
                RB(hT0f[:, 0:1024], ("xr", 2), hk0), RB(hT0f[:, 1024:2048], ("xr", 3), hk0)]
        zb_r = [RB(halves(cbuf)[i], ("zb", i), [("cbuf", 2 * i), ("cbuf", 2 * i + 1)]) for i in range(2)] + \
               [RB(halves(cn)[i], ("zb", 2 + i), [("cn", 2 * i), ("cn", 2 * i + 1)]) for i in range(2)]
        gb_r = [RB(halves(s_az)[i], ("gb", i), [("s_az", 2 * i), ("s_az", 2 * i + 1)]) for i in range(2)] + \
               [RB(hT1f[:, 0:1024], ("gb", 2), hk1), RB(hT1f[:, 1024:2048], ("gb", 3), hk1)]
        h2bf_r = [RB(sig[i][:].bitcast(BF16), ("h2bf", i), [("sig", i)]) for i in range(2)] + \
                 [RB(s_ln[i][:].bitcast(BF16), ("h2bf", 2 + i), [("s_ln", i)]) for i in range(2)]
        h2T_r = [RB(ub[i][:].bitcast(BF16), ("h2T", i), [("ub", i)]) for i in range(2)] + \
                [RB(gvb[i][:].bitcast(BF16), ("h2T", 2 + i), [("gvb", i)]) for i in range(2)]
        pb_r = [RB(szb[i // 2][:, (i % 2) * 256:(i % 2) * 256 + 256], ("pbuf", i), [("szb", i // 2)]) for i in range(4)]
        pT_r = [RB(vb[i][:, 0:256], ("pT", i), [("vb", i)]) for i in range(2)] + \
               [RB(ybb[i][:, 0:256], ("pT", 2 + i), [("ybb", i)]) for i in range(2)]
        TB = es.enter_context(nc.psum_tensor("TB", [128, 2, 1024], BF16))
        G = es.enter_context(nc.psum_tensor("G", [128, 6, 512], F32))

        em = Em(nc)
        pe, act, dve, pool, sp = nc.tensor, nc.scalar, nc.vector, nc.gpsimd, nc.sync

        ag_b = bc[:, 0:1024]
        pg_b = bc[:, 1024:2048]
        pb_b = bc[:, 2048:3072]
        sg_b = bc[:, 3072:3584]
        sb_b = bc[:, 3584:4096]
        ab_rows = rows2[:, 0:1024]
        bg_rows = rows2[:, 1024:2048]
        bs_rows = rows2[:, 2048:3072]
        WA_ALL = [("wA", i) for i in range(12)]
        WB_ALL = [("wB", i) for i in range(3)]
        LA_VAL, LA_GATE, LA_Z = 0, 512, 1024
        LB_U, LB_V, LB_Z = 0, 512, 1024

        rr = {"bn": 0, "stt": 0, "clk": 0}
        g_last = [0] * 6
        tb_last = [0, 0]

        def tick():
            rr["clk"] += 1
            return rr["clk"]

        g_busy = [False] * 6

        def galloc(n=1):
            if n == 2:
                pairs = sorted(((max(g_last[2 * i], g_last[2 * i + 1]), 2 * i) for i in range(3)
                                if not (g_busy[2 * i] or g_busy[2 * i + 1])))
                assert pairs, "no free PSUM bank pair"
                b = pairs[0][1]
                t = tick()
                g_last[b] = g_last[b + 1] = t
                g_busy[b] = g_busy[b + 1] = True
                return b
            order = sorted((b for b in range(6) if not g_busy[b]), key=lambda b: g_last[b])
            assert len(order) >= n, "no free PSUM bank"
            t = tick()
            if n == 1:
                g_last[order[0]] = t
                g_busy[order[0]] = True
                return order[0]
            sel = sorted(order[:n])
            for b in sel:
                g_last[b] = t
                g_busy[b] = True
            return sel

        def gfree(*banks):
            t = tick()
            for b in banks:
                assert g_busy[b]
                g_busy[b] = False
                g_last[b] = t

        def gtouch(*banks):
            pass

        def tballoc():
            t = 0 if tb_last[0] <= tb_last[1] else 1
            tb_last[t] = tick()
            return t

        def load_small():
            em.op("sp", lambda: sp.dma_start(out=colp[:], in_=colp_d), writes=["colp"], dma="colp")
            em.op("sp", lambda: sp.dma_start(out=cw[:], in_=cw_d), writes=["cw"], dma="cw")
        import os
        WORDER = os.environ.get("WORDER", "GAB")
        US_PER_MIB = 4.2
        wstate = {"mib": 0.0}

        def wdma(fn_, wkeys, name, mib, after=()):
            wstate["mib"] += mib
            em.op("pool", fn_, reads=list(after), writes=wkeys, dma=name, c=3.0 + US_PER_MIB * wstate["mib"])

        def w_glu(qs=(0, 1, 2, 3), after=()):
            for q in qs:
                for kb in (q, 4 + q):
                    wdma(lambda kb=kb: pool.dma_start(out=wA[:, kb], in_=wA_d[:, kb]),
                         [("wA", kb)], f"wA{kb}", 0.5, after)

        def w_az(after=()):
            for q in range(4):
                wdma(lambda q=q: pool.dma_start(out=wA[:, 8 + q], in_=wA_d[:, 8 + q]),
                     [("wA", 8 + q)], f"wA{8 + q}", 0.5, after)

        def w_b(after=()):
            for i in (2, 1, 0):
                wdma(lambda i=i: pool.dma_start(out=wB[:, i, 0:4], in_=wB_d[:, i, 0:4]),
                     [("wB", i, 0)], f"wB{i}", 1.0, after)
                wdma(lambda i=i: pool.dma_start(out=wB[:, i, 4:8], in_=wB_d[:, i, 4:8]),
                     [("wB", i)], f"wB{i}", 1.0, after)
        w_glu(qs=(0,))

        def weights_rest():
            w_glu(qs=(1, 2, 3), after=[("st1", 2)])
            w_b(after=[("st1", 3)])
            w_az(after=[("st1", 4)])
            wdma(lambda: pool.dma_start(out=wmT[:], in_=wsT_d), ["wmT"], "wmT", 0.5, [("st1", 4)])
        em.op("pool", lambda: pool.memset(ident_b[:], 0.0), writes=["ident_b"])
        em.op("pool", lambda: pool.affine_select(out=ident_b[:], in_=ident_b[:], compare_op=ALU.not_equal, fill=1.0,
                                                 base=0, pattern=[[-1, 128]], channel_multiplier=1),
              reads=["ident_b"], writes=["ident_b"])
        em.op("pool", lambda: pool.memset(ident_f[:], 0.0), writes=["ident_f"])
        em.op("pool", lambda: pool.affine_select(out=ident_f[:], in_=ident_f[:], compare_op=ALU.not_equal, fill=1.0,
                                                 base=0, pattern=[[-1, 128]], channel_multiplier=1),
              reads=["ident_f"], writes=["ident_f"])
        em.op("pool", lambda: pool.memset(neghalf[:, 0:1], -0.5), writes=["neghalf"])
        em.op("pool", lambda: pool.memset(neghalf[:, 1:2], -1.0), writes=["neghalf"])
        em.op("pool", lambda: pool.memset(ones2[:], 1.0), writes=["ones2"])
        em.op("pool", lambda: pool.memset(rows2[:], 0.0), writes=["rows2"])

        def setup_late():
            LATE = [("hT", 0, 3)]
            em.op("sp", lambda: sp.dma_start(out=bc[:, 3072:4096], in_=bc_d[:, 3072:4096]), reads=[("st1", 2)], writes=["bc_s"], dma="bc_s", c=6.0)
            for i in range(4):
                em.op("dve", lambda i=i: dve.tensor_copy(out=i32[32 * i:32 * i + 32, :],
                                                         in_=ident_f[32 * i:32 * i + 32, 32 * i:32 * i + 32]),
                      reads=LATE + ["ident_f"], writes=["i32"])
            for j in range(4):
                em.op("dve", lambda j=j: dve.scalar_tensor_tensor(
                    out=dgw[:, :, j, :], in0=cw[:, j, :].unsqueeze(2).broadcast_to([128, NTAP, 32]),
                    scalar=0.5, in1=i32[:].unsqueeze(1).broadcast_to([128, NTAP, 32]),
                    op0=ALU.mult, op1=ALU.mult), reads=LATE + ["cw", "i32"], writes=["dgw"])
            em.op("pool", lambda: pool.affine_select(out=wmT[:], in_=wmT[:], compare_op=ALU.is_ge, fill=0.0, base=0,
                                                     pattern=[[0, 8], [1, 128]], channel_multiplier=-1),
                  reads=["wmT"], writes=["wmT"])
            t_r = [cbuf[0:2, 0:2, :].rearrange("p a b -> p (a b)"), cbuf[0:2, 2:4, :].rearrange("p a b -> p (a b)"),
                   cn[0:2, 0:2, :].rearrange("p a b -> p (a b)")]
            t_k = [[("cbuf", 0), ("cbuf", 1)], [("cbuf", 2), ("cbuf", 3)], [("cn", 0), ("cn", 1)]]
            for si in range(3):
                em.op("sp", lambda si=si: sp.dma_start(out=t_r[si], in_=rows_d[:, si * 1024:(si + 1) * 1024]), writes=t_k[si], dma=f"rows{si}")
            em.op("dve", lambda: dve.tensor_scalar(out=t_r[0], in0=t_r[0], scalar1=ALPHA, scalar2=None, op0=ALU.mult),
                  reads=LATE + t_k[0], writes=t_k[0])
            hi = s_ln[0][:].bitcast(BF16)[0:2, :]
            lo = s_ln[1][:].bitcast(BF16)[0:2, :]
            hi32 = cn[0:2, 2:4, :].rearrange("p a b -> p (a b)")
            for si in range(3):
                dst = rows2[0:2, si * 1024:(si + 1) * 1024]
                r32 = t_r[si]
                key = t_k[si]
                em.op("dve", lambda r32=r32: dve.tensor_copy(out=hi, in_=r32), reads=LATE + key, writes=[("s_ln", 0)])
                em.op("dve", lambda: dve.tensor_copy(out=hi32, in_=hi), reads=[("s_ln", 0)], writes=[("cn", 2), ("cn", 3)])
                em.op("dve", lambda r32=r32: dve.tensor_tensor(out=hi32, in0=r32, in1=hi32, op=ALU.subtract),
                      reads=key + [("cn", 2), ("cn", 3)], writes=[("cn", 2), ("cn", 3)])
                em.op("dve", lambda: dve.tensor_scalar(out=lo, in0=hi32, scalar1=colp[0:2, C_M1:C_M1 + 1], scalar2=None, op0=ALU.mult),
                      reads=[("cn", 2), ("cn", 3), "colp"], writes=[("s_ln", 1)])
                em.op("dve", lambda dst=dst: dve.scalar_tensor_tensor(out=dst, in0=hi, scalar=colp[0:2, C_M0:C_M0 + 1], in1=lo,
                                                                      op0=ALU.mult, op1=ALU.add),
                      reads=[("s_ln", 0), ("s_ln", 1), "colp"], writes=["rows2"])

        def ln_stats(src_aps, src_keys, stat_ap, stat_key, need_nmr=True):
            bi = rr["bn"]
            rr["bn"] = (bi + 1) % 8
            n = len(src_aps)
            for j, s in enumerate(src_aps):
                em.op("dve", lambda j=j, s=s: dve.bn_stats(out=bnst[:, bi, j, :], in_=s), reads=src_keys, writes=[("bnst", bi)])
            em.op("dve", lambda: dve.bn_aggr(out=stat_ap[:, 0:2], in_=bnst[:, bi, 0:n, :].rearrange("p a b -> p (a b)")),
                  reads=[("bnst", bi)], writes=[stat_key], n=1)
            em.op("dve", lambda: dve.tensor_scalar(out=stat_ap[:, 3:4], in0=stat_ap[:, 1:2], scalar1=EPS, scalar2=None,
                                                   op0=ALU.add), reads=[stat_key], writes=[stat_key], n=1)
            em.op("pool", lambda: pool.tensor_tensor(out=stat_ap[:, 2:3], in0=stat_ap[:, 3:4], in1=neghalf[:, 0:1], op=ALU.pow),
                  reads=[stat_key, "neghalf"], writes=[stat_key], c=0.5)
            if need_nmr:
                em.op("dve", lambda: dve.scalar_tensor_tensor(out=stat_ap[:, 3:4], in0=stat_ap[:, 0:1], scalar=-1.0,
                                                              in1=stat_ap[:, 2:3], op0=ALU.mult, op1=ALU.mult),
                      reads=[stat_key], writes=[stat_key], n=1)

        def tmp_stat():
            i = rr["stt"]
            rr["stt"] = (i + 1) % 16
            return stt_[:, i, :], ("stt", i)

        def s1_load(src_d, row0, sidx):
            slot = sidx % RS
            xt = xb[slot]
            xk = ("xb", slot)
            em.op("sp", lambda: sp.dma_start(out=xt[:], in_=src_d[row0:row0 + 128, :]), writes=[xk], dma=f"xb{slot}", n=1024)
            sa = st1[:, sidx, :]
            sk = ("st1", sidx)
            ln_stats([xt[:, 0:512], xt[:, 512:1024]], [xk], sa, sk)
            em.op("act", lambda: act.activation(out=xhat[slot][:], in_=xt[:], func=AF.Identity, bias=sa[:, 3:4], scale=sa[:, 2:3]),
                  reads=[xk, sk], writes=[("xhat", slot)], n=1024)

        def s1_trans(sidx, hs, col0):
            slot = sidx % RS
            tb = tballoc()
            for k in range(8):
                em.op("pe", lambda k=k: pe.transpose(out=TB[:, tb, k * 128:(k + 1) * 128], in_=xhat[slot][:, k * 128:(k + 1) * 128],
                                                     identity=ident_b[:]),
                      reads=[("xhat", slot), "ident_b"], writes=[("TB", tb)])
            hk = ("hT", hs, col0 // 128)
            for k in range(8):
                if k % 2 == 0:
                    em.op("act", lambda k=k: act.activation(out=hT[hs][:, k, col0:col0 + 128], in_=TB[:, tb, k * 128:(k + 1) * 128],
                                                            func=AF.Identity, bias=colp[:, C_BE + k:C_BE + k + 1],
                                                            scale=colp[:, C_GE + k:C_GE + k + 1]),
                          reads=[("TB", tb), "colp"], writes=[hk], n=128)
                else:
                    em.op("dve", lambda k=k: dve.tensor_scalar(out=hT[hs][:, k, col0:col0 + 128], in0=TB[:, tb, k * 128:(k + 1) * 128],
                                                               scalar1=colp[:, C_GE + k:C_GE + k + 1], scalar2=colp[:, C_BE + k:C_BE + k + 1],
                                                               op0=ALU.mult, op1=ALU.add),
                          reads=[("TB", tb), "colp"], writes=[hk], n=128)

        def gen_S1(st):
            hs = st % 2
            base = st * 4
            for s in range(min(RS - 1, 4)):
                s1_load(x_d, (base + s) * 128, 1 + base + s)
                yield
            for s in range(4):
                if s + RS - 1 < 4:
                    s1_load(x_d, (base + s + RS - 1) * 128, 1 + base + s + RS - 1)
                    yield
                s1_trans(1 + base + s, hs, s * 128)
                yield

        def proj_fm(bank, loc_off, kb, n, hs, hkeys):
            for k in range(8):
                em.op("pe", lambda k=k: pe.matmul(G[:, bank, 0:n], lhsT=wA[:, kb, k, :], rhs=hT[hs][:, k, 0:n],
                                                  start=(k == 0), stop=(k == 7)),
                      reads=[("wA", kb)] + hkeys, writes=[("G", bank)], n=n)

        def glu_chunk(q, n, hs, hkeys, dst_ap, dkey, src_lo):
            bv = galloc(2)
            bg = bv + 1
            proj_fm(bv, LA_VAL + q * 128, q, n, hs, hkeys)
            proj_fm(bg, LA_GATE + q * 128, 4 + q, n, hs, hkeys)
            sl = q % 2
            em.op("act", lambda: act.activation(out=sig[sl][:, 0:n], in_=G[:, bg, 0:n], func=AF.Tanh, scale=0.5),
                  reads=[("G", bg)], writes=[("sig", sl)], n=n)
            em.op("dve", lambda: dve.scalar_tensor_tensor(out=dst_ap, in0=sig[sl][:, src_lo:n], scalar=1.0, in1=G[:, bv, src_lo:n],
                                                          op0=ALU.add, op1=ALU.mult),
                  reads=[("sig", sl), ("G", bv)], writes=[dkey], n=n - src_lo)
            gfree(bv, bg)

        def az_chunk(q, hs, hkeys):
            bz = galloc(1)
            proj_fm(bz, LA_Z + q * 128, 8 + q, ST, hs, hkeys)
            em.op("act", lambda: act.activation(out=s_az[:, q, :], in_=G[:, bz, :], func=AF.Silu),
                  reads=[("G", bz)], writes=[("s_az", q)], tbl="silu")
            gfree(bz)

        def emit_pass2_wA():
            o_wo = em.op("pool", lambda: pool.dma_start(out=w_out[:, 0:2, :], in_=w_out_d[:, 0:2, :]), writes=WA_ALL + ["w_out"], dma="w_out", c=9.0)
            for kk in (2, 4, 6):
                o_n = em.op("pool", lambda kk=kk: pool.dma_start(out=w_out[:, kk:kk + 2, :], in_=w_out_d[:, kk:kk + 2, :]), writes=["w_out"],
                            dma="w_out", c=9.0 + 2.1 * kk)
                o_n.deps = list(o_wo.deps)
            o_wp = em.op("pool", lambda: pool.dma_start(out=w_ple, in_=w_ple_d), writes=["w_ple"], dma="w_ple", c=30.0)
            o_wp.deps = list(o_wo.deps)

        def gen_Ahead(st):
            hs = st % 2
            asl = st % 2
            hall = [("hT", hs, s) for s in range(4)]
            akeys = [("a_buf", asl, q) for q in range(4)]
            for q in range(4):
                glu_chunk(q, ST, hs, hall, a_buf[asl][:, q, 32:32 + ST], ("a_buf", asl, q), 0)
                yield

        def head_copy(st):
            asl = st % 2
            akeys = [("a_buf", asl, q) for q in range(4)]
            nkeys = [("a_buf", 1 - asl, q) for q in range(4)]
            em.op("pool", lambda: pool.tensor_copy(out=a_buf[1 - asl][:, :, 0:32], in_=a_buf[asl][:, :, ST:ST + 32]),
                  reads=akeys, writes=nkeys)

        def gen_Atail(st):
            asl = st % 2
            akeys = [("a_buf", asl, q) for q in range(4)]
            for half in range(2):
                gpair = galloc(2)
                gc = {2 * half: gpair, 2 * half + 1: gpair + 1}
                for tap in range(NTAP):
                    for i in (2 * half, 2 * half + 1):
                        for j in range(4):
                            em.op("pe", lambda tap=tap, i=i, j=j, gb_=gc[i]: pe.matmul(
                                G[32 * j:32 * j + 32, gb_, :], lhsT=dgw[32 * i:32 * i + 32, tap, j, :],
                                rhs=a_buf[asl][32 * i:32 * i + 32, j, 2 + tap:2 + tap + ST],
                                start=(tap == 0), stop=(tap == NTAP - 1), tile_position=(32 * i, 32 * j)),
                                reads=["dgw"] + akeys, writes=[("G", gc[i])], c=0.034, tbl="c32")
                    if tap % 8 == 7:
                        yield
                for q in (2 * half, 2 * half + 1):
                    em.op("act", lambda q=q, gb_=gc[q]: act.activation(out=cbuf[:, q, :], in_=G[:, gb_, :], func=AF.Identity,
                                                                       bias=colp[:, C_CB + q:C_CB + q + 1], scale=1.0),
                          reads=[("G", gc[q]), "colp"], writes=[("cbuf", q)])
                gfree(gpair, gpair + 1)
                yield

            def fwd(s):
                gt = galloc(1)
                for q in range(4):
                    em.op("pe", lambda q=q: pe.transpose(out=G[:, gt, q * 128:(q + 1) * 128], in_=cbuf[:, q, s * 128:(s + 1) * 128],
                                                         identity=ident_f[:]),
                          reads=[("cbuf", q), "ident_f"], writes=[("G", gt)], c=0.15)
                sa, sk = tmp_stat()
                ln_stats([G[:, gt, :]], [("G", gt)], sa, sk)
                em.op("act", lambda: act.activation(out=cn[:, s, :], in_=G[:, gt, :], func=AF.Identity, bias=sa[:, 3:4], scale=sa[:, 2:3]),
                      reads=[("G", gt), sk], writes=[("cn", s)])
                gfree(gt)
            for s in range(4):
                fwd(s)
                yield
            yield
            yield
            for q in range(4):
                gbk = galloc(1)
                for s in range(4):
                    em.op("pe", lambda s=s, q=q, gbk=gbk: pe.transpose(out=G[:, gbk, s * 128:(s + 1) * 128], in_=cn[:, s, q * 128:(q + 1) * 128],
                                                                       identity=ident_f[:]),
                          reads=[("cn", s), "ident_f"], writes=[("G", gbk)], c=0.15)
                sl = q % 2
                em.op("act", lambda q=q, sl=sl, gbk=gbk: act.activation(out=s_ln[sl][:], in_=G[:, gbk, :], func=AF.Silu,
                                                                        bias=colp[:, C_LB + q:C_LB + q + 1], scale=colp[:, C_LG + q:C_LG + q + 1]),
                      reads=[("G", gbk), "colp"], writes=[("s_ln", sl)], tbl="silu")
                gfree(gbk)
                em.op("dve", lambda q=q, sl=sl: dve.tensor_tensor(out=y_all[:, q, st * ST:(st + 1) * ST], in0=s_ln[sl][:], in1=s_az[:, q, :], op=ALU.mult),
                      reads=[("s_ln", sl), ("s_az", q)], writes=[("ya", st, q)])
                if st + 1 < NST:
                    nh = (st + 1) % 2
                    az_chunk(q, nh, [("hT", nh, s) for s in range(4)])
                yield

        def B_P(st, s):
            hs = st % 2
            c0 = s * 128
            bs_ = s % 2
            hk = [("hT", hs, s)]
            order = (("z", LB_Z, 2), ("v", LB_V, 1), ("u", LB_U, 0)) if s % 2 == 0 else (("v", LB_V, 1), ("u", LB_U, 0), ("z", LB_Z, 2))
            dsts = {"z": (szb[bs_], ("szb", bs_), AF.Silu), "v": (gvb[bs_], ("gvb", bs_), AF.Gelu), "u": (ub[bs_], ("ub", bs_), AF.Gelu)}
            for name, off, kb in order:
                b = galloc(1)
                for k in range(8):
                    em.op("pe", lambda k=k, b=b, kb=kb: pe.matmul(G[:, b, :], lhsT=hT[hs][:, k, c0:c0 + 128], rhs=wB[:, kb, k, :],
                                                                    start=(k == 0), stop=(k == 7)),
                          reads=[("wB", kb)] + hk, writes=[("G", b)])
                dt_, dk_, fn_ = dsts[name]
                em.op("act", lambda b=b, dt_=dt_, fn_=fn_: act.activation(out=dt_[:], in_=G[:, b, :], func=fn_), reads=[("G", b)], writes=[dk_],
                      tbl=("silu" if name == "z" else "gelu"))
                gfree(b)
            sa, sk = tmp_stat()
            ln_stats([gvb[bs_][:]], [("gvb", bs_)], sa, sk)
            em.op("dve", lambda: dve.scalar_tensor_tensor(out=gvb[bs_][:], in0=gvb[bs_][:], scalar=sa[:, 0:1], in1=sg_b, op0=ALU.subtract, op1=ALU.mult),
                  reads=[("gvb", bs_), sk, "bc_s"], writes=[("gvb", bs_)])
            em.op("dve", lambda: dve.scalar_tensor_tensor(out=vb[bs_][:], in0=gvb[bs_][:], scalar=sa[:, 2:3], in1=sb_b, op0=ALU.mult, op1=ALU.add),
                  reads=[("gvb", bs_), sk, "bc_s"], writes=[("vb", bs_)])
            em.op("pool", lambda: pool.tensor_tensor(out=ub[bs_][:], in0=ub[bs_][:], in1=szb[bs_][:], op=ALU.mult),
                  reads=[("ub", bs_), ("szb", bs_)], writes=[("ub", bs_)])

        def B_Q(st, s):
            bs_ = s % 2
            gs = st * 4 + s
            bsg = galloc(1)
            for h in range(8):
                em.op("pe", lambda h=h: pe.matmul(G[:, bsg, h * 64:(h + 1) * 64], lhsT=wmT[:, h, :], rhs=vb[bs_][:, h * 64:(h + 1) * 64],
                                                  start=True, stop=False), reads=["wmT", ("vb", bs_)], writes=[("G", bsg)], c=0.03)
                em.op("pe", lambda h=h: pe.matmul(G[:, bsg, h * 64:(h + 1) * 64], lhsT=bs_rows[:, h * 128:(h + 1) * 128], rhs=ones2[:, 0:64],
                                                  start=False, stop=True), reads=["rows2", "ones2"], writes=[("G", bsg)], c=0.03)
            em.op("dve", lambda: dve.tensor_tensor(out=ybb[bs_][:], in0=G[:, bsg, :], in1=ub[bs_][:], op=ALU.mult),
                  reads=[("G", bsg), ("ub", bs_)], writes=[("ybb", bs_)])
            gfree(bsg)

        def B_Q2(st, s):
            bs_ = s % 2
            gs = st * 4 + s
            tb = tballoc()
            for e in range(4):
                em.op("pe", lambda e=e: pe.transpose(out=TB[:, tb, e * 128:(e + 1) * 128], in_=ybb[bs_][:, e * 128:(e + 1) * 128], identity=ident_b[:]),
                      reads=[("ybb", bs_), "ident_b"], writes=[("TB", tb)])
            em.op("act", lambda: act.copy(out=y_all[:, 4:8, gs * 128:(gs + 1) * 128], in_=TB[:, tb, 0:512].rearrange("p (e t) -> p e t", e=4)),
                  reads=[("TB", tb)], writes=[("yb", gs)])

        def gen_B(st):
            B_P(st, 0)
            yield
            B_P(st, 1)
            yield
            B_Q(st, 0)
            yield
            B_P(st, 2)
            yield
            B_Q2(st, 0)
            B_Q(st, 1)
            yield
            B_P(st, 3)
            yield
            B_Q2(st, 1)
            B_Q(st, 2)
            yield
            yield
            B_Q2(st, 2)
            B_Q(st, 3)
            yield
            yield
            B_Q2(st, 3)
            yield
            if st == NST - 1:
                o_wg = em.op("pool", lambda: pool.dma_start(out=w_gate[:, 0:2, :], in_=w_gate_d[:, 0:2, :]), writes=WB_ALL + ["w_gate"],
                             dma="w_gate", c=9.0)
                for kk in (2, 4, 6):
                    o_n = em.op("pool", lambda kk=kk: pool.dma_start(out=w_gate[:, kk:kk + 2, :], in_=w_gate_d[:, kk:kk + 2, :]), writes=["w_gate"],
                                dma="w_gate", c=9.0 + 2.1 * kk)
                    o_n.deps = list(o_wg.deps)

        P2 = {}
        P2V = int(os.environ.get("P2V", "0"))

        def p2_ctx(gs):
            if gs not in P2:
                sl = gs % R
                P2[gs] = dict(sl=sl, r0=gs * 128, xr=xr_r[sl], zb=zb_r[sl], pbu=pb_r[sl], pTt=pT_r[sl], h2b=h2bf_r[sl],
                              h2t=h2T_r[sl], gbf=gb_r[sl], pf=pbf[gs % 2], pfk=("pbf", gs % 2),
                              sa1=st1[:, gs + 1, :], sk1=("st1", gs + 1))
            return P2[gs]

        def p2_loads(gs):
            c = p2_ctx(gs)
            em.op("sp", lambda: sp.dma_start(out=c["xr"].ap, in_=x_d[c["r0"]:c["r0"] + 128, :]), writes=c["xr"].wkeys(), dma=f"xr{c['sl']}", n=1024)
            em.op("sp", lambda: sp.dma_start(out=c["pbu"].ap, in_=p_d[c["r0"]:c["r0"] + 128, :]), writes=c["pbu"].wkeys(), dma=f"pbuf{c['sl']}", n=256)

        def p2_ppath(gs):
            c = p2_ctx(gs)
            pf, pfk, pbu, pTt = c["pf"], c["pfk"], c["pbu"], c["pTt"]
            em.op("act", lambda: act.copy(out=pf[:], in_=pbu.ap), reads=[pbu.key], writes=[pfk], n=256)
            tb2 = tballoc()
            for j in range(2):
                em.op("pe", lambda j=j: pe.transpose(out=TB[:, tb2, j * 128:(j + 1) * 128], in_=pf[:, j * 128:(j + 1) * 128], identity=ident_b[:]),
                      reads=[pfk, "ident_b"], writes=[("TB", tb2)])
            em.op("act", lambda: act.copy(out=pTt.ap, in_=TB[:, tb2, 0:256]), reads=[("TB", tb2)], writes=pTt.wkeys(), n=256)

        def p2_h2T(gs):
            c = p2_ctx(gs)
            h2b, h2t = c["h2b"], c["h2t"]
            tb = tballoc()
            for k in range(8):
                em.op("pe", lambda k=k: pe.transpose(out=TB[:, tb, k * 128:(k + 1) * 128], in_=h2b.ap[:, k * 128:(k + 1) * 128], identity=ident_b[:]),
                      reads=[h2b.key, "ident_b"], writes=[("TB", tb)])
            em.op("act", lambda: act.copy(out=h2t.ap, in_=TB[:, tb, :]), reads=[("TB", tb)], writes=h2t.wkeys(), n=1024)

        def p2_xprep(gs):
            c = p2_ctx(gs)
            xr, sa1, sk1 = c["xr"], c["sa1"], c["sk1"]
            if P2V >= 1:
                em.op("act", lambda: act.activation(out=xr.ap, in_=xr.ap, func=AF.Identity, bias=sa1[:, 3:4], scale=sa1[:, 2:3]),
                      reads=[xr.key, sk1], writes=[xr.key], n=1024)
                em.op("pool", lambda: pool.tensor_tensor(out=xr.ap, in0=xr.ap, in1=ag_b, op=ALU.mult), reads=[xr.key, "bc_l"], writes=[xr.key], n=1024)
                return
            em.op("dve", lambda: dve.scalar_tensor_tensor(out=xr.ap, in0=xr.ap, scalar=sa1[:, 0:1], in1=ag_b, op0=ALU.subtract, op1=ALU.mult),
                  reads=[xr.key, sk1, "bc_l"], writes=[xr.key], n=1024)

        def p2_mix(gs):
            c = p2_ctx(gs)
            r0 = c["r0"]
            st = gs // 4
            yk = [("ya", st, q) for q in range(4)] + [("yb", gs)]
            gm = galloc(2)
            c["gm"] = gm
            for nb in range(2):
                for e in range(8):
                    em.op("pe", lambda nb=nb, e=e: pe.matmul(G[:, gm + nb, :], lhsT=y_all[:, e, r0:r0 + 128], rhs=w_out[:, e, nb * 512:(nb + 1) * 512],
                                                             start=(e == 0), stop=False),
                          reads=["w_out"] + yk, writes=[("G", gm + nb)])
                em.op("pe", lambda nb=nb: pe.matmul(G[:, gm + nb, :], lhsT=ones2[:], rhs=ab_rows[:, nb * 512:(nb + 1) * 512],
                                                    start=False, stop=True), reads=["ones2", "rows2"], writes=[("G", gm + nb)])

        def p2_ln(gs):
            c = p2_ctx(gs)
            xr, zb, h2b, sa1, sk1, gm = c["xr"], c["zb"], c["h2b"], c["sa1"], c["sk1"], c["gm"]
            if P2V >= 1:
                em.op("dve", lambda: dve.tensor_tensor(out=zb.ap.rearrange("p (a b) -> p a b", a=2), in0=xr.ap.rearrange("p (a b) -> p a b", a=2),
                                                       in1=G[:, gm:gm + 2, :], op=ALU.add),
                      reads=[xr.key, ("G", gm), ("G", gm + 1)], writes=zb.wkeys(), n=1024)
            else:
                em.op("dve", lambda: dve.scalar_tensor_tensor(out=zb.ap.rearrange("p (a b) -> p a b", a=2), in0=xr.ap.rearrange("p (a b) -> p a b", a=2),
                                                              scalar=sa1[:, 2:3], in1=G[:, gm:gm + 2, :], op0=ALU.mult, op1=ALU.add),
                      reads=[xr.key, sk1, ("G", gm), ("G", gm + 1)], writes=zb.wkeys(), n=1024)
            gfree(gm, gm + 1)
            sa, sk = tmp_stat()
            ln_stats([zb.ap[:, 0:512], zb.ap[:, 512:1024]], [zb.key], sa, sk, need_nmr=(P2V >= 2))
            if P2V >= 2:
                em.op("act", lambda: act.activation(out=zb.ap, in_=zb.ap, func=AF.Identity, bias=sa[:, 3:4], scale=sa[:, 2:3]),
                      reads=[zb.key, sk], writes=[zb.key], n=1024)
                em.op("pool", lambda: pool.tensor_tensor(out=zb.ap, in0=zb.ap, in1=pg_b, op=ALU.mult), reads=[zb.key, "bc_l"], writes=[zb.key], n=1024)
                em.op("dve", lambda: dve.tensor_tensor(out=zb.ap, in0=zb.ap, in1=pb_b, op=ALU.add), reads=[zb.key, "bc_l"], writes=[zb.key], n=1024)
            else:
                em.op("dve", lambda: dve.scalar_tensor_tensor(out=zb.ap, in0=zb.ap, scalar=sa[:, 0:1], in1=pg_b, op0=ALU.subtract, op1=ALU.mult),
                      reads=[zb.key, sk, "bc_l"], writes=[zb.key], n=1024)
                em.op("dve", lambda: dve.scalar_tensor_tensor(out=zb.ap, in0=zb.ap, scalar=sa[:, 2:3], in1=pb_b, op0=ALU.mult, op1=ALU.add),
                      reads=[zb.key, sk, "bc_l"], writes=[zb.key], n=1024)
            em.op("act", lambda: act.copy(out=h2b.ap, in_=zb.ap), reads=[zb.key], writes=h2b.wkeys(), n=1024)

        def p2_gate(gs):
            c = p2_ctx(gs)
            pTt, h2t, gbf = c["pTt"], c["h2t"], c["gbf"]
            gp = galloc(2)
            for nb in range(2):
                for j in range(2):
                    em.op("pe", lambda nb=nb, j=j: pe.matmul(G[:, gp + nb, :], lhsT=pTt.ap[:, j * 128:(j + 1) * 128], rhs=w_ple[:, j, nb * 512:(nb + 1) * 512],
                                                             start=(j == 0), stop=(j == 1)),
                          reads=["w_ple", pTt.key], writes=[("G", gp + nb)])
            gg = galloc(2)
            for nb in range(2):
                for k in range(8):
                    em.op("pe", lambda nb=nb, k=k: pe.matmul(G[:, gg + nb, :], lhsT=h2t.ap[:, k * 128:(k + 1) * 128], rhs=w_gate[:, k, nb * 512:(nb + 1) * 512],
                                                             start=(k == 0), stop=False),
                          reads=["w_gate", h2t.key], writes=[("G", gg + nb)])
                em.op("pe", lambda nb=nb: pe.matmul(G[:, gg + nb, :], lhsT=ones2[:], rhs=bg_rows[:, nb * 512:(nb + 1) * 512],
                                                    start=False, stop=True), reads=["ones2", "rows2"], writes=[("G", gg + nb)])
            g3 = gbf.ap.rearrange("p (a b) -> p a b", a=2)
            em.op("act", lambda: act.activation(out=g3, in_=G[:, gg:gg + 2, :], func=AF.Tanh, scale=0.5),
                  reads=[("G", gg), ("G", gg + 1)], writes=gbf.wkeys(), n=1024)
            gfree(gg, gg + 1)
            em.op("dve", lambda: dve.scalar_tensor_tensor(out=g3, in0=g3, scalar=1.0, in1=G[:, gp:gp + 2, :], op0=ALU.add, op1=ALU.mult),
                  reads=[gbf.key, ("G", gp), ("G", gp + 1)], writes=[gbf.key], n=1024)
            gfree(gp, gp + 1)

        def p2_out(gs):
            c = p2_ctx(gs)
            gbf, zb, r0 = c["gbf"], c["zb"], c["r0"]
            em.op("pool", lambda: pool.tensor_tensor(out=gbf.ap, in0=gbf.ap, in1=zb.ap, op=ALU.add), reads=[gbf.key, zb.key], writes=[gbf.key], n=1024)
            return em.op("sp", lambda: sp.dma_start(out=out_d[r0:r0 + 128, :], in_=gbf.ap), reads=[gbf.key], writes=[gbf.key], dma=f"out{c['sl']}", n=1024)

        load_small()
        _interleave([gen_S1(0)])
        weights_rest()
        s1_load(xh_d, 0, 0)
        s1_trans(0, 1, 0)
        setup_late()
        for q in range(4):
            glu_chunk(q, 128, 1, [("hT", 1, 0)], a_buf[0][:, q, 0:32], ("a_buf", 0, q), 96)
        em.op("dve", lambda: dve.tensor_scalar(out=a_buf[0][:, :, 0:32], in0=a_buf[0][:, :, 0:32], scalar1=colp[:, C_HM:C_HM + 1], scalar2=None,
                                               op0=ALU.mult),
              reads=[("a_buf", 0, q) for q in range(4)] + ["colp"], writes=[("a_buf", 0, q) for q in range(4)])
        for q in range(4):
            az_chunk(q, 0, [("hT", 0, s) for s in range(4)])
        for st in range(NST):
            _interleave([gen_Ahead(st), gen_B(st), gen_Atail(st - 1) if st >= 1 else None,
                         gen_S1(st + 1) if st + 1 < NST else None])
            if st + 1 < NST:
                head_copy(st)
            if st == 1:
                em.op("sp", lambda: sp.dma_start(out=bc[:, 0:3072], in_=bc_d[:, 0:3072]), reads=[("ya", 0, 0)], writes=["bc_l"], dma="bc_l", c=10.0)
                em.op("dve", lambda: dve.tensor_scalar(out=ag_b, in0=ag_b, scalar1=ALPHA, scalar2=None, op0=ALU.mult),
                      reads=["bc_l"], writes=["bc_l"], n=1024)
        emit_pass2_wA()
        _interleave([gen_Atail(NST - 1)])
        last_out = []
        em.op("dve", lambda: dve.tensor_scalar(out=w_ple, in0=w_ple, scalar1=0.5, scalar2=None, op0=ALU.mult),
              reads=["w_ple"], writes=["w_ple"])
        ok = lambda i: 0 <= i < NSUB
        p2_loads(0)
        for t in range(NSUB + 3):
            if ok(t - 3):
                last_out.append(p2_out(t - 3))
            if ok(t + 1):
                p2_loads(t + 1)
            if ok(t):
                p2_ppath(t)
            if ok(t - 2):
                p2_h2T(t - 2)
            if ok(t):
                p2_xprep(t)
            if ok(t - 1):
                p2_ln(t - 1)
            if ok(t):
                p2_mix(t)
            if ok(t - 2):
                p2_gate(t - 2)
        nwait = em.finalize(es, final=last_out)
        import os
        if os.environ.get("KVERB"):
            print("sched estimate us:", getattr(em, "est", None), "waits:", nwait, "ops:", len(em.ops))
    return nc


def _perm():
    perm = np.zeros(512, dtype=np.int64)
    for qo in range(4):
        for j in range(4):
            for r in range(32):
                perm[qo * 128 + 32 * j + r] = j * 128 + 32 * qo + r
    return perm


_NC_CACHE = {}


def kernel(x, p, ln_emb_g, ln_emb_b, w_in, conv_w, conv_b, conv_ln_g, conv_ln_b, sgu_ln_g, sgu_ln_b, w_s, b_s,
           w_out, post_ln_g, post_ln_b, w_ple, w_ple_gate, b_ple_gate):
    f = lambda a: np.ascontiguousarray(np.asarray(a, dtype=np.float32))
    x, p = f(x), f(p)
    perm = _perm()
    w_in0 = f(w_in)[0]
    colsA = np.concatenate([np.arange(0, 1024), A_Z + perm])
    colsB = np.arange(1536, 3072)
    w_inA_l = np.ascontiguousarray(w_in0[:, colsA].reshape(8, 128, 12, 128).transpose(1, 2, 0, 3))
    w_inB_l = np.ascontiguousarray(w_in0[:, colsB].reshape(8, 128, 3, 512).transpose(1, 2, 0, 3))
    w_out0 = f(w_out)[0]
    rowsel = np.arange(1024)
    rowsel[0:512] = perm
    w_out_l = np.ascontiguousarray(w_out0[rowsel, :].reshape(8, 128, 1024).transpose(1, 0, 2))
    w_gate_l = np.ascontiguousarray(f(w_ple_gate)[0].reshape(8, 128, 1024).transpose(1, 0, 2))
    w_ple_l = np.ascontiguousarray(f(w_ple)[0].reshape(2, 128, 1024).transpose(1, 0, 2))
    wsT = np.ascontiguousarray(f(w_s)[0].transpose(2, 0, 1))
    cw = np.ascontiguousarray(f(conv_w)[0].T.reshape(4, 128, NTAP).transpose(1, 0, 2))
    colp = np.zeros((128, NCOL), np.float32)
    colp[:, C_GE:C_GE + 8] = f(ln_emb_g).reshape(8, 128).T
    colp[:, C_BE:C_BE + 8] = f(ln_emb_b).reshape(8, 128).T
    colp[:, C_CB:C_CB + 4] = f(conv_b)[0][perm].reshape(4, 128).T
    colp[:, C_LG:C_LG + 4] = f(conv_ln_g)[0][perm].reshape(4, 128).T
    colp[:, C_LB:C_LB + 4] = f(conv_ln_b)[0][perm].reshape(4, 128).T
    colp[0, C_M0] = 1.0
    colp[1, C_M1] = 1.0
    bcrow = np.concatenate([f(ln_emb_g), f(post_ln_g)[0], f(post_ln_b)[0], f(sgu_ln_g)[0], f(sgu_ln_b)[0]])
    bc = np.ascontiguousarray(np.broadcast_to(bcrow[None, :], (128, 4096)))
    rrow = np.concatenate([f(ln_emb_b), f(b_ple_gate)[0], f(b_s)[0].reshape(-1)])
    rows = np.ascontiguousarray(np.broadcast_to(rrow[None, :], (2, 3072)))

    B, S, _ = x.shape
    per_b = S // NT
    in_maps = []
    for c in range(NCORES):
        b, j = divmod(c, per_b)
        t0 = j * NT
        xc = x[b, t0:t0 + NT]
        pc = p[0, b, t0:t0 + NT]
        cp = colp.copy()
        if j == 0:
            xh = np.zeros((128, D), np.float32)
            cp[:, C_HM] = 0.0
        else:
            xh = x[b, t0 - 128:t0]
            cp[:, C_HM] = 1.0
        in_maps.append({"x": np.ascontiguousarray(xc), "xh": np.ascontiguousarray(xh), "p": np.ascontiguousarray(pc),
                        "w_inA": w_inA_l, "w_inB": w_inB_l, "w_out": w_out_l, "w_gate": w_gate_l, "w_ple": w_ple_l, "wsT": wsT, "cw": cw,
                        "colp": cp, "bc": bc, "rows": rows})
    nc = build_program()
    res = run_bass_kernel_spmd(nc, in_maps, core_ids=list(range(NCORES)))
    out = np.empty((B, S, D), np.float32)
    for c in range(NCORES):
        b, j = divmod(c, per_b)
        out[b, j * NT:(j + 1) * NT] = res.results[c]["out"]
    return out
```

```python
import numpy as np
from contextlib import ExitStack
import concourse.bass as bass
import concourse.mybir as mybir
from concourse.bass_utils import run_bass_kernel_spmd

F32 = mybir.dt.float32
BF16 = mybir.dt.bfloat16
AF = mybir.ActivationFunctionType
ALU = mybir.AluOpType

NCORES = 8
D = 1024
NT = 2048
NSUB = NT // 128
ST = 512
NST = NT // ST
DPLE = 256
ALPHA = float(2.0 ** 0.25)
EPS = 1e-5
A_VAL, A_GATE, A_Z, B_U, B_V, B_Z = 0, 512, 1024, 1536, 2048, 2560
NTAP = 31
C_GE, C_BE, C_CB, C_LG, C_LB, C_HM, C_M0, C_M1, NCOL = 0, 8, 16, 20, 24, 28, 29, 30, 32


class Op:
    __slots__ = ("eng", "fn", "dma", "deps", "signaled", "tok", "clock", "c", "tbl", "idx", "end", "nin", "users", "rdy")

    def __init__(self, eng, fn, dma):
        self.eng, self.fn, self.dma = eng, fn, dma
        self.deps = ()
        self.signaled = False
        self.tok = None
        self.clock = None


DEF_COST = {"pe": 0.27, "act": 0.8, "dve": 0.8, "pool": 1.5, "sp": 0.4}


class Em:
    def __init__(self, nc):
        self.nc = nc
        self.ops = []
        self.last_w = {}
        self.readers = {}
        self.h = {"pe": nc.tensor, "act": nc.scalar, "dve": nc.vector, "pool": nc.gpsimd, "sp": nc.sync}

    @staticmethod
    def cost(eng, fn, n, dma):
        names = fn.__code__.co_names
        if dma is not None:
            return 2.0 + n * 128 * 4 / 150e3
        if eng == "pe":
            return 0.1 if "transpose" in names else 0.02 + n * 0.00042
        if eng == "act":
            return 0.35 + n / 1400.0
        if eng == "dve":
            if "bn_stats" in names:
                return 0.06 + n / 800.0
            if "bn_aggr" in names:
                return 0.2
            return 0.1 + n / 870.0
        if eng == "pool":
            if "tensor_tensor" in names:
                return 0.3 + n * 0.002
            return 0.5
        return 0.4

    def op(self, eng, fn, reads=(), writes=(), dma=None, c=None, tbl=None, n=512):
        o = Op(eng, fn, dma)
        o.c = self.cost(eng, fn, n, dma) if c is None else c
        if eng == "pe" and tbl is None:
            tbl = "full"
        if eng == "act" and tbl is None:
            names = fn.__code__.co_consts
        o.tbl = tbl
        deps = []
        seen = set()

        def add(d):
            if d is not None and id(d) not in seen:
                seen.add(id(d))
                deps.append(d)

        for k in reads:
            add(self.last_w.get(k))
        for k in writes:
            add(self.last_w.get(k))
            for r in self.readers.get(k, ()):
                add(r)
        o.deps = deps
        for k in reads:
            self.readers.setdefault(k, []).append(o)
        for k in writes:
            self.last_w[k] = o
            self.readers[k] = []
        self.ops.append(o)
        return o

    def schedule(self):
        import heapq
        ops = self.ops
        for i, o in enumerate(ops):
            o.idx = i
            o.users = []
            o.nin = 0
            o.rdy = 0.0
            o.end = None
        for o in ops:
            for d in o.deps:
                d.users.append(o)
                o.nin += 1
        import os
        POL = os.environ.get("SCHED_POL", "fifo")
        SLACK = float(os.environ.get("SCHED_SLACK", "0.3"))
        bl = [0.0] * len(ops)
        for o in reversed(ops):
            m = 0.0
            for u in o.users:
                if bl[u.idx] > m:
                    m = bl[u.idx]
            bl[o.idx] = o.c + m
        ready = {e: [] for e in self.h}
        for o in ops:
            if o.nin == 0:
                heapq.heappush(ready[o.eng], (o.idx, o))
        free_at = {e: 0.0 for e in self.h}
        cur_tbl = [None]
        cur_mode = [None]
        HOP = float(os.environ.get("SCHED_HOP", "0.15"))
        SCALE = {"pe": float(os.environ.get("SCHED_PE", "1.1")), "act": float(os.environ.get("SCHED_ACT", "1.0")),
                 "dve": float(os.environ.get("SCHED_DVE", "1.15")), "pool": float(os.environ.get("SCHED_POOL", "1.0")), "sp": 1.0}
        order = []
        n = len(ops)
        LOOK = int(os.environ.get("SCHED_LOOK", "20"))
        while len(order) < n:
            best = None
            for e, hp in ready.items():
                if not hp:
                    continue
                cands = heapq.nsmallest(LOOK, hp)
                for idx, o in cands:
                    st = max(free_at[e], o.rdy)
                    pen = 0.0
                    if e == "act" and o.tbl is not None and o.tbl != cur_tbl[0]:
                        pen = 1.3
                    if e == "pe" and o.tbl is not None and o.tbl != cur_mode[0]:
                        pen = 0.25
                    if POL == "bl":
                        key = (round((st + pen) / SLACK), 1 if (e == "pool" and o.dma is not None) else 0, -bl[idx], idx)
                    else:
                        key = (st + pen, 1 if (e == "pool" and o.dma is not None) else 0, idx)
                    if best is None or key < best[0]:
                        best = (key, o, st + pen)
            _, o, st = best
            e = o.eng
            ready[e] = [(i, x) for (i, x) in ready[e] if x is not o]
            heapq.heapify(ready[e])
            if o.dma is not None:
                free_at[e] = st + (1.1 if e == "pool" else 0.45)
                o.end = st + o.c
            else:
                free_at[e] = st + o.c * SCALE[e]
                o.end = free_at[e]
            if e == "act" and o.tbl is not None:
                cur_tbl[0] = o.tbl
            if e == "pe" and o.tbl is not None:
                cur_mode[0] = o.tbl
            order.append(o)
            for u in o.users:
                u.rdy = max(u.rdy, o.end + (0.0 if (u.eng == e and o.dma is None) else HOP))
                u.nin -= 1
                if u.nin == 0:
                    heapq.heappush(ready[u.eng], (u.idx, u))
        self.ops = order
        return max(o.end for o in order)

    def finalize(self, es, final=()):
        nc = self.nc
        import os
        if not os.environ.get("NOSCHED"):
            self.est = self.schedule()
        pos = {id(o): i for i, o in enumerate(self.ops)}
        wdeps = {}
        for o in self.ops:
            rep = {}
            lst = []
            for d in o.deps:
                if d.dma is not None:
                    lst.append(d)
                    continue
                if d.eng == "pe" and o.eng == "pe" and o.dma is None:
                    continue
                r = rep.get(d.eng)
                if r is None or pos[id(d)] > pos[id(r)]:
                    rep[d.eng] = d
            for d in rep.values():
                d.signaled = True
                lst.append(d)
            wdeps[id(o)] = lst
        cnt = {}
        sems = {}

        def sem(key):
            if key not in sems:
                sems[key] = es.enter_context(nc.semaphore("s_" + "_".join(str(x) for x in key)))
            return sems[key]

        for o in self.ops:
            if o.dma is not None:
                key = ("dma", o.dma)
                cnt[key] = cnt.get(key, 0) + 16
                o.tok = (key, cnt[key])
            elif o.signaled:
                key = ("eng", o.eng)
                cnt[key] = cnt.get(key, 0) + 1
                o.tok = (key, cnt[key])
        known = {e: {} for e in self.h}
        nwait = 0
        EMBED_ENGS = ("act", "dve", "pool") if os.environ.get("NO_EMBED_PE") else ("act", "dve", "pool", "pe")
        for o in self.ops:
            E = o.eng
            kn = known[E]
            need = {}
            for d in wdeps[id(o)]:
                s, v = d.tok
                if kn.get(s, 0) >= v:
                    continue
                if need.get(s, 0) < v:
                    need[s] = v
            embed = None
            items = [(s, v) for s, v in need.items() if kn.get(s, 0) < v]
            if items and E in EMBED_ENGS and o.dma is None:
                embed = items.pop()
            for s, v in items:
                self.h[E].wait_ge(sem(s), v)
                nwait += 1
                kn[s] = v
            if embed is not None:
                kn[embed[0]] = embed[1]
            for d in wdeps[id(o)]:
                if d.clock is not None:
                    for s, v in d.clock.items():
                        if kn.get(s, 0) < v:
                            kn[s] = v
            ins = o.fn()
            if embed is not None:
                ins._wait_ge(sem(embed[0]), embed[1])
            if o.dma is not None:
                ins.then_inc(sem(o.tok[0]), 16)
            elif o.signaled:
                ins.then_inc(sem(o.tok[0]), 1)
            if o.tok is not None:
                ck = dict(kn)
                ck[o.tok[0]] = max(ck.get(o.tok[0], 0), o.tok[1])
                o.clock = ck
        fin = {}
        for o in final:
            s, v = o.tok
            fin[s] = max(fin.get(s, 0), v)
        for s, v in fin.items():
            self.h["sp"].wait_ge(sem(s), v)
        return nwait


def _interleave(gens):
    gens = [g for g in gens if g is not None]
    import os
    if os.environ.get("SEQ"):
        for g in gens:
            for _ in g:
                pass
        return
    while gens:
        nxt = []
        for g in gens:
            try:
                next(g)
                nxt.append(g)
            except StopIteration:
                pass
        gens = nxt


def build_program():
    nc = bass.Bass("TRN2", target_bir_lowering=False)
    dt = nc.dram_tensor
    x_d = dt("x", [NT, D], F32, kind="ExternalInput").ap()
    xh_d = dt("xh", [128, D], F32, kind="ExternalInput").ap()
    p_d = dt("p", [NT, DPLE], F32, kind="ExternalInput").ap()
    wA_d = dt("w_inA", [128, 12, 8, 128], F32, kind="ExternalInput").ap()
    wB_d = dt("w_inB", [128, 3, 8, 512], F32, kind="ExternalInput").ap()
    w_out_d = dt("w_out", [128, 8, D], F32, kind="ExternalInput").ap()
    w_gate_d = dt("w_gate", [128, 8, D], F32, kind="ExternalInput").ap()
    w_ple_d = dt("w_ple", [128, 2, D], F32, kind="ExternalInput").ap()
    wsT_d = dt("wsT", [128, 8, 128], F32, kind="ExternalInput").ap()
    cw_d = dt("cw", [128, 4, NTAP], F32, kind="ExternalInput").ap()
    colp_d = dt("colp", [128, NCOL], F32, kind="ExternalInput").ap()
    bc_d = dt("bc", [128, 4096], F32, kind="ExternalInput").ap()
    rows_d = dt("rows", [2, 3072], F32, kind="ExternalInput").ap()
    out_d = dt("out", [NT, D], F32, kind="ExternalOutput").ap()

    es = ExitStack()
    with es:
        def sb(name, shape, dtype):
            return es.enter_context(nc.sbuf_tensor(name, shape, dtype))

        wA = sb("wA", [128, 12, 8, 128], BF16)
        wB = sb("wB", [128, 3, 8, 512], BF16)
        wAf = wA[:].rearrange("p b k c -> p (b k c)")
        wBf = wB[:].rearrange("p b k c -> p (b k c)")
        w_out = wAf[:, 0:8192].rearrange("p (k c) -> p k c", k=8)
        w_ple = wAf[:, 8192:10240].rearrange("p (k c) -> p k c", k=2)
        w_gate = wBf[:, 0:8192].rearrange("p (k c) -> p k c", k=8)
        y_all = sb("y_all", [128, 8, NT], BF16)
        wmT = sb("wmT", [128, 8, 128], BF16)
        dgw = sb("dgw", [128, NTAP, 4, 32], BF16)
        cw = sb("cw_s", [128, 4, NTAP], F32)
        colp = sb("colp_s", [128, NCOL], F32)
        bc = sb("bc_s", [128, 4096], F32)
        rows2 = sb("rows2", [128, 3072], BF16)
        ones2 = sb("ones2", [128, 128], BF16)
        ident_b = sb("ident_b", [128, 128], BF16)
        ident_f = sb("ident_f", [128, 128], F32)
        i32 = sb("i32", [128, 32], F32)
        neghalf = sb("neghalf", [128, 2], F32)
        st1 = sb("st1", [128, NSUB + 1, 4], F32)
        stt_ = sb("stt_", [128, 16, 4], F32)
        bnst = sb("bnst", [128, 8, 2, 6], F32)
        import os
        RS = int(os.environ.get("RS", "2"))
        xb = [sb(f"xb{i}", [128, D], F32) for i in range(RS)]
        xhat = [sb(f"xhat{i}", [128, D], BF16) for i in range(RS)]
        hT = [sb(f"hT{i}", [128, 8, ST], BF16) for i in range(2)]
        a_buf = [sb(f"a_buf{i}", [128, 4, 32 + ST], BF16) for i in range(2)]
        sig = [sb(f"sig{i}", [128, ST], F32) for i in range(2)]
        s_az = sb("s_az", [128, 4, ST], F32)
        s_ln = [sb(f"s_ln{i}", [128, ST], F32) for i in range(2)]
        cbuf = sb("cbuf", [128, 4, ST], F32)
        cn = sb("cn", [128, 4, 512], F32)
        ub = [sb(f"ub{i}", [128, 512], F32) for i in range(2)]
        gvb = [sb(f"gvb{i}", [128, 512], F32) for i in range(2)]
        szb = [sb(f"szb{i}", [128, 512], F32) for i in range(2)]
        vb = [sb(f"vb{i}", [128, 512], BF16) for i in range(2)]
        ybb = [sb(f"ybb{i}", [128, 512], BF16) for i in range(2)]
        pbf = [sb(f"pbf{i}", [128, DPLE], BF16) for i in range(2)]

        class RB:
            def __init__(self, ap, key, legacy):
                self.ap, self.key, self.legacy, self.first = ap, key, list(legacy), True

            def wkeys(self):
                if self.first:
                    self.first = False
                    return [self.key] + self.legacy
                return [self.key]

        def halves(t3):
            return [t3[:, 0:2, :].rearrange("p a b -> p (a b)"), t3[:, 2:4, :].rearrange("p a b -> p (a b)")]
        hT0f = hT[0][:].rearrange("p k c -> p (k c)").bitcast(F32)
        hT1f = hT[1][:].rearrange("p k c -> p (k c)").bitcast(F32)
        hk0 = [("hT", 0, s) for s in range(4)]
        hk1 = [("hT", 1, s) for s in range(4)]
        R = 4
        xr_r = [RB(xb[0][:], ("xb", 0), []), RB(xb[1][:], ("xb", 1), []),
                RB(hT0f[:, 0:1024], ("xr", 2), hk0), RB(hT0f[:, 1024:2048], ("xr", 3), hk0)]
        zb_r = [RB(halves(cbuf)[i], ("zb", i), [("cbuf", 2 * i), ("cbuf", 2 * i + 1)]) for i in range(2)] + \
               [RB(halves(cn)[i], ("zb", 2 + i), [("cn", 2 * i), ("cn", 2 * i + 1)]) for i in range(2)]
        gb_r = [RB(halves(s_az)[i], ("gb", i), [("s_az", 2 * i), ("s_az", 2 * i + 1)]) for i in range(2)] + \
               [RB(hT1f[:, 0:1024], ("gb", 2), hk1), RB(hT1f[:, 1024:2048], ("gb", 3), hk1)]
        h2bf_r = [RB(sig[i][:].bitcast(BF16), ("h2bf", i), [("sig", i)]) for i in range(2)] + \
                 [RB(s_ln[i][:].bitcast(BF16), ("h2bf", 2 + i), [("s_ln", i)]) for i in range(2)]
        h2T_r = [RB(ub[i][:].bitcast(BF16), ("h2T", i), [("ub", i)]) for i in range(2)] + \
                [RB(gvb[i][:].bitcast(BF16), ("h2T", 2 + i), [("gvb", i)]) for i in range(2)]
        pb_r = [RB(szb[i // 2][:, (i % 2) * 256:(i % 2) * 256 + 256], ("pbuf", i), [("szb", i // 2)]) for i in range(4)]
        pT_r = [RB(vb[i][:, 0:256], ("pT", i), [("vb", i)]) for i in range(2)] + \
               [RB(ybb[i][:, 0:256], ("pT", 2 + i), [("ybb", i)]) for i in range(2)]
        TB = es.enter_context(nc.psum_tensor("TB", [128, 2, 1024], BF16))
        G = es.enter_context(nc.psum_tensor("G", [128, 6, 512], F32))

        em = Em(nc)
        pe, act, dve, pool, sp = nc.tensor, nc.scalar, nc.vector, nc.gpsimd, nc.sync

        ag_b = bc[:, 0:1024]
        pg_b = bc[:, 1024:2048]
        pb_b = bc[:, 2048:3072]
        sg_b = bc[:, 3072:3584]
        sb_b = bc[:, 3584:4096]
        ab_rows = rows2[:, 0:1024]
        bg_rows = rows2[:, 1024:2048]
        bs_rows = rows2[:, 2048:3072]
        WA_ALL = [("wA", i) for i in range(12)]
        WB_ALL = [("wB", i) for i in range(3)]
        LA_VAL, LA_GATE, LA_Z = 0, 512, 1024
        LB_U, LB_V, LB_Z = 0, 512, 1024

        rr = {"bn": 0, "stt": 0, "clk": 0}
        g_last = [0] * 6
        tb_last = [0, 0]

        def tick():
            rr["clk"] += 1
            return rr["clk"]

        g_busy = [False] * 6

        def galloc(n=1):
            if n == 2:
                pairs = sorted(((max(g_last[2 * i], g_last[2 * i + 1]), 2 * i) for i in range(3)
                                if not (g_busy[2 * i] or g_busy[2 * i + 1])))
                assert pairs, "no free PSUM bank pair"
                b = pairs[0][1]
                t = tick()
                g_last[b] = g_last[b + 1] = t
                g_busy[b] = g_busy[b + 1] = True
                return b
            order = sorted((b for b in range(6) if not g_busy[b]), key=lambda b: g_last[b])
            assert len(order) >= n, "no free PSUM bank"
            t = tick()
            if n == 1:
                g_last[order[0]] = t
                g_busy[order[0]] = True
                return order[0]
            sel = sorted(order[:n])
            for b in sel:
                g_last[b] = t
                g_busy[b] = True
            return sel

        def gfree(*banks):
            t = tick()
            for b in banks:
                assert g_busy[b]
                g_busy[b] = False
                g_last[b] = t

        def gtouch(*banks):
            pass

        def tballoc():
            t = 0 if tb_last[0] <= tb_last[1] else 1
            tb_last[t] = tick()
            return t

        def load_small():
            em.op("sp", lambda: sp.dma_start(out=colp[:], in_=colp_d), writes=["colp"], dma="colp")
            em.op("sp", lambda: sp.dma_start(out=cw[:], in_=cw_d), writes=["cw"], dma="cw")
        import os
        WORDER = os.environ.get("WORDER", "GAB")
        US_PER_MIB = 4.2
        wstate = {"mib": 0.0}

        def wdma(fn_, wkeys, name, mib, after=()):
            wstate["mib"] += mib
            em.op("pool", fn_, reads=list(after), writes=wkeys, dma=name, c=3.0 + US_PER_MIB * wstate["mib"])

        def w_glu(qs=(0, 1, 2, 3), after=()):
            for q in qs:
                for kb in (q, 4 + q):
                    wdma(lambda kb=kb: pool.dma_start(out=wA[:, kb], in_=wA_d[:, kb]),
                         [("wA", kb)], f"wA{kb}", 0.5, after)

        def w_az(after=()):
            for q in range(4):
                wdma(lambda q=q: pool.dma_start(out=wA[:, 8 + q], in_=wA_d[:, 8 + q]),
                     [("wA", 8 + q)], f"wA{8 + q}", 0.5, after)

        def w_b(after=()):
            for i in (2, 1, 0):
                wdma(lambda i=i: pool.dma_start(out=wB[:, i, 0:4], in_=wB_d[:, i, 0:4]),
                     [("wB", i, 0)], f"wB{i}", 1.0, after)
                wdma(lambda i=i: pool.dma_start(out=wB[:, i, 4:8], in_=wB_d[:, i, 4:8]),
                     [("wB", i)], f"wB{i}", 1.0, after)
        w_glu(qs=(0,))

        def weights_rest():
            w_glu(qs=(1, 2, 3), after=[("st1", 2)])
            if WORDER == "GAB":
                w_az(after=[("st1", 3)])
                w_b(after=[("st1", 4)])
            else:
                w_b(after=[("st1", 3)])
                w_az(after=[("st1", 4)])
            wdma(lambda: pool.dma_start(out=wmT[:], in_=wsT_d), ["wmT"], "wmT", 0.5, [("st1", 4)])
        em.op("pool", lambda: pool.memset(ident_b[:], 0.0), writes=["ident_b"])
        em.op("pool", lambda: pool.affine_select(out=ident_b[:], in_=ident_b[:], compare_op=ALU.not_equal, fill=1.0,
                                                 base=0, pattern=[[-1, 128]], channel_multiplier=1),
              reads=["ident_b"], writes=["ident_b"])
        em.op("pool", lambda: pool.memset(ident_f[:], 0.0), writes=["ident_f"])
        em.op("pool", lambda: pool.affine_select(out=ident_f[:], in_=ident_f[:], compare_op=ALU.not_equal, fill=1.0,
                                                 base=0, pattern=[[-1, 128]], channel_multiplier=1),
              reads=["ident_f"], writes=["ident_f"])
        em.op("pool", lambda: pool.memset(neghalf[:, 0:1], -0.5), writes=["neghalf"])
        em.op("pool", lambda: pool.memset(neghalf[:, 1:2], -1.0), writes=["neghalf"])
        em.op("pool", lambda: pool.memset(ones2[:], 1.0), writes=["ones2"])
        em.op("pool", lambda: pool.memset(rows2[:], 0.0), writes=["rows2"])

        def setup_late():
            LATE = [("hT", 0, 3)]
            em.op("sp", lambda: sp.dma_start(out=bc[:, 3072:4096], in_=bc_d[:, 3072:4096]), reads=[("st1", 2)], writes=["bc_s"], dma="bc_s", c=6.0)
            for i in range(4):
                em.op("dve", lambda i=i: dve.tensor_copy(out=i32[32 * i:32 * i + 32, :],
                                                         in_=ident_f[32 * i:32 * i + 32, 32 * i:32 * i + 32]),
                      reads=LATE + ["ident_f"], writes=["i32"])
            for j in range(4):
                em.op("dve", lambda j=j: dve.scalar_tensor_tensor(
                    out=dgw[:, :, j, :], in0=cw[:, j, :].unsqueeze(2).broadcast_to([128, NTAP, 32]),
                    scalar=0.5, in1=i32[:].unsqueeze(1).broadcast_to([128, NTAP, 32]),
                    op0=ALU.mult, op1=ALU.mult), reads=LATE + ["cw", "i32"], writes=["dgw"])
            em.op("pool", lambda: pool.affine_select(out=wmT[:], in_=wmT[:], compare_op=ALU.is_ge, fill=0.0, base=0,
                                                     pattern=[[0, 8], [1, 128]], channel_multiplier=-1),
                  reads=["wmT"], writes=["wmT"])
            t_r = [cbuf[0:2, 0:2, :].rearrange("p a b -> p (a b)"), cbuf[0:2, 2:4, :].rearrange("p a b -> p (a b)"),
                   cn[0:2, 0:2, :].rearrange("p a b -> p (a b)")]
            t_k = [[("cbuf", 0), ("cbuf", 1)], [("cbuf", 2), ("cbuf", 3)], [("cn", 0), ("cn", 1)]]
            for si in range(3):
                em.op("sp", lambda si=si: sp.dma_start(out=t_r[si], in_=rows_d[:, si * 1024:(si + 1) * 1024]), writes=t_k[si], dma=f"rows{si}")
            em.op("dve", lambda: dve.tensor_scalar(out=t_r[0], in0=t_r[0], scalar1=ALPHA, scalar2=None, op0=ALU.mult),
                  reads=LATE + t_k[0], writes=t_k[0])
            hi = s_ln[0][:].bitcast(BF16)[0:2, :]
            lo = s_ln[1][:].bitcast(BF16)[0:2, :]
            hi32 = cn[0:2, 2:4, :].rearrange("p a b -> p (a b)")
            for si in range(3):
                dst = rows2[0:2, si * 1024:(si + 1) * 1024]
                r32 = t_r[si]
                key = t_k[si]
                em.op("dve", lambda r32=r32: dve.tensor_copy(out=hi, in_=r32), reads=LATE + key, writes=[("s_ln", 0)])
                em.op("dve", lambda: dve.tensor_copy(out=hi32, in_=hi), reads=[("s_ln", 0)], writes=[("cn", 2), ("cn", 3)])
                em.op("dve", lambda r32=r32: dve.tensor_tensor(out=hi32, in0=r32, in1=hi32, op=ALU.subtract),
                      reads=key + [("cn", 2), ("cn", 3)], writes=[("cn", 2), ("cn", 3)])
                em.op("dve", lambda: dve.tensor_scalar(out=lo, in0=hi32, scalar1=colp[0:2, C_M1:C_M1 + 1], scalar2=None, op0=ALU.mult),
                      reads=[("cn", 2), ("cn", 3), "colp"], writes=[("s_ln", 1)])
                em.op("dve", lambda dst=dst: dve.scalar_tensor_tensor(out=dst, in0=hi, scalar=colp[0:2, C_M0:C_M0 + 1], in1=lo,
                                                                      op0=ALU.mult, op1=ALU.add),
                      reads=[("s_ln", 0), ("s_ln", 1), "colp"], writes=["rows2"])

        def ln_stats(src_aps, src_keys, stat_ap, stat_key, need_nmr=True):
            bi = rr["bn"]
            rr["bn"] = (bi + 1) % 8
            n = len(src_aps)
            for j, s in enumerate(src_aps):
                em.op("dve", lambda j=j, s=s: dve.bn_stats(out=bnst[:, bi, j, :], in_=s), reads=src_keys, writes=[("bnst", bi)])
            em.op("dve", lambda: dve.bn_aggr(out=stat_ap[:, 0:2], in_=bnst[:, bi, 0:n, :].rearrange("p a b -> p (a b)")),
                  reads=[("bnst", bi)], writes=[stat_key], n=1)
            em.op("dve", lambda: dve.tensor_scalar(out=stat_ap[:, 3:4], in0=stat_ap[:, 1:2], scalar1=EPS, scalar2=None,
                                                   op0=ALU.add), reads=[stat_key], writes=[stat_key], n=1)
            em.op("pool", lambda: pool.tensor_tensor(out=stat_ap[:, 2:3], in0=stat_ap[:, 3:4], in1=neghalf[:, 0:1], op=ALU.pow),
                  reads=[stat_key, "neghalf"], writes=[stat_key], c=0.5)
            if need_nmr:
                em.op("dve", lambda: dve.scalar_tensor_tensor(out=stat_ap[:, 3:4], in0=stat_ap[:, 0:1], scalar=-1.0,
                                                              in1=stat_ap[:, 2:3], op0=ALU.mult, op1=ALU.mult),
                      reads=[stat_key], writes=[stat_key], n=1)

        def tmp_stat():
            i = rr["stt"]
            rr["stt"] = (i + 1) % 16
            return stt_[:, i, :], ("stt", i)

        def s1_load(src_d, row0, sidx):
            slot = sidx % RS
            xt = xb[slot]
            xk = ("xb", slot)
            em.op("sp", lambda: sp.dma_start(out=xt[:], in_=src_d[row0:row0 + 128, :]), writes=[xk], dma=f"xb{slot}", n=1024)
            sa = st1[:, sidx, :]
            sk = ("st1", sidx)
            ln_stats([xt[:, 0:512], xt[:, 512:1024]], [xk], sa, sk)
            em.op("act", lambda: act.activation(out=xhat[slot][:], in_=xt[:], func=AF.Identity, bias=sa[:, 3:4], scale=sa[:, 2:3]),
                  reads=[xk, sk], writes=[("xhat", slot)], n=1024)

        def s1_trans(sidx, hs, col0):
            slot = sidx % RS
            tb = tballoc()
            for k in range(8):
                em.op("pe", lambda k=k: pe.transpose(out=TB[:, tb, k * 128:(k + 1) * 128], in_=xhat[slot][:, k * 128:(k + 1) * 128],
                                                     identity=ident_b[:]),
                      reads=[("xhat", slot), "ident_b"], writes=[("TB", tb)])
            hk = ("hT", hs, col0 // 128)
            for k in range(8):
                if k % 2 == 0:
                    em.op("act", lambda k=k: act.activation(out=hT[hs][:, k, col0:col0 + 128], in_=TB[:, tb, k * 128:(k + 1) * 128],
                                                            func=AF.Identity, bias=colp[:, C_BE + k:C_BE + k + 1],
                                                            scale=colp[:, C_GE + k:C_GE + k + 1]),
                          reads=[("TB", tb), "colp"], writes=[hk], n=128)
                else:
                    em.op("dve", lambda k=k: dve.tensor_scalar(out=hT[hs][:, k, col0:col0 + 128], in0=TB[:, tb, k * 128:(k + 1) * 128],
                                                               scalar1=colp[:, C_GE + k:C_GE + k + 1], scalar2=colp[:, C_BE + k:C_BE + k + 1],
                                                               op0=ALU.mult, op1=ALU.add),
                          reads=[("TB", tb), "colp"], writes=[hk], n=128)

        def gen_S1(st):
            hs = st % 2
            base = st * 4
            for s in range(min(RS - 1, 4)):
                s1_load(x_d, (base + s) * 128, 1 + base + s)
                yield
            for s in range(4):
                if s + RS - 1 < 4:
                    s1_load(x_d, (base + s + RS - 1) * 128, 1 + base + s + RS - 1)
                    yield
                s1_trans(1 + base + s, hs, s * 128)
                yield

        def proj_fm(bank, loc_off, kb, n, hs, hkeys):
            for k in range(8):
                em.op("pe", lambda k=k: pe.matmul(G[:, bank, 0:n], lhsT=wA[:, kb, k, :], rhs=hT[hs][:, k, 0:n],
                                                  start=(k == 0), stop=(k == 7)),
                      reads=[("wA", kb)] + hkeys, writes=[("G", bank)], n=n)

        def glu_chunk(q, n, hs, hkeys, dst_ap, dkey, src_lo):
            bv = galloc(2)
            bg = bv + 1
            proj_fm(bv, LA_VAL + q * 128, q, n, hs, hkeys)
            proj_fm(bg, LA_GATE + q * 128, 4 + q, n, hs, hkeys)
            sl = q % 2
            em.op("act", lambda: act.activation(out=sig[sl][:, 0:n], in_=G[:, bg, 0:n], func=AF.Tanh, scale=0.5),
                  reads=[("G", bg)], writes=[("sig", sl)], n=n)
            em.op("dve", lambda: dve.scalar_tensor_tensor(out=dst_ap, in0=sig[sl][:, src_lo:n], scalar=1.0, in1=G[:, bv, src_lo:n],
                                                          op0=ALU.add, op1=ALU.mult),
                  reads=[("sig", sl), ("G", bv)], writes=[dkey], n=n - src_lo)
            gfree(bv, bg)

        def az_chunk(q, hs, hkeys):
            bz = galloc(1)
            proj_fm(bz, LA_Z + q * 128, 8 + q, ST, hs, hkeys)
            em.op("act", lambda: act.activation(out=s_az[:, q, :], in_=G[:, bz, :], func=AF.Silu),
                  reads=[("G", bz)], writes=[("s_az", q)], tbl="silu")
            gfree(bz)

        def emit_pass2_wA():
            o_wo = em.op("pool", lambda: pool.dma_start(out=w_out[:, 0:2, :], in_=w_out_d[:, 0:2, :]), writes=WA_ALL + ["w_out"], dma="w_out", c=9.0)
            for kk in (2, 4, 6):
                o_n = em.op("pool", lambda kk=kk: pool.dma_start(out=w_out[:, kk:kk + 2, :], in_=w_out_d[:, kk:kk + 2, :]), writes=["w_out"],
                            dma="w_out", c=9.0 + 2.1 * kk)
                o_n.deps = list(o_wo.deps)
            o_wp = em.op("pool", lambda: pool.dma_start(out=w_ple, in_=w_ple_d), writes=["w_ple"], dma="w_ple", c=30.0)
            o_wp.deps = list(o_wo.deps)

        def gen_Ahead(st):
            hs = st % 2
            asl = st % 2
            hall = [("hT", hs, s) for s in range(4)]
            akeys = [("a_buf", asl, q) for q in range(4)]
            for q in range(4):
                glu_chunk(q, ST, hs, hall, a_buf[asl][:, q, 32:32 + ST], ("a_buf", asl, q), 0)
                yield

        def head_copy(st):
            asl = st % 2
            akeys = [("a_buf", asl, q) for q in range(4)]
            nkeys = [("a_buf", 1 - asl, q) for q in range(4)]
            em.op("pool", lambda: pool.tensor_copy(out=a_buf[1 - asl][:, :, 0:32], in_=a_buf[asl][:, :, ST:ST + 32]),
                  reads=akeys, writes=nkeys)

        def gen_Atail(st):
            asl = st % 2
            akeys = [("a_buf", asl, q) for q in range(4)]
            for half in range(2):
                gpair = galloc(2)
                gc = {2 * half: gpair, 2 * half + 1: gpair + 1}
                for tap in range(NTAP):
                    for i in (2 * half, 2 * half + 1):
                        for j in range(4):
                            em.op("pe", lambda tap=tap, i=i, j=j, gb_=gc[i]: pe.matmul(
                                G[32 * j:32 * j + 32, gb_, :], lhsT=dgw[32 * i:32 * i + 32, tap, j, :],
                                rhs=a_buf[asl][32 * i:32 * i + 32, j, 2 + tap:2 + tap + ST],
                                start=(tap == 0), stop=(tap == NTAP - 1), tile_position=(32 * i, 32 * j)),
                                reads=["dgw"] + akeys, writes=[("G", gc[i])], c=0.034, tbl="c32")
                    if tap % 8 == 7:
                        yield
                for q in (2 * half, 2 * half + 1):
                    em.op("act", lambda q=q, gb_=gc[q]: act.activation(out=cbuf[:, q, :], in_=G[:, gb_, :], func=AF.Identity,
                                                                       bias=colp[:, C_CB + q:C_CB + q + 1], scale=1.0),
                          reads=[("G", gc[q]), "colp"], writes=[("cbuf", q)])
                gfree(gpair, gpair + 1)
                yield

            def fwd(s):
                gt = galloc(1)
                for q in range(4):
                    em.op("pe", lambda q=q: pe.transpose(out=G[:, gt, q * 128:(q + 1) * 128], in_=cbuf[:, q, s * 128:(s + 1) * 128],
                                                         identity=ident_f[:]),
                          reads=[("cbuf", q), "ident_f"], writes=[("G", gt)], c=0.15)
                sa, sk = tmp_stat()
                ln_stats([G[:, gt, :]], [("G", gt)], sa, sk)
                em.op("act", lambda: act.activation(out=cn[:, s, :], in_=G[:, gt, :], func=AF.Identity, bias=sa[:, 3:4], scale=sa[:, 2:3]),
                      reads=[("G", gt), sk], writes=[("cn", s)])
                gfree(gt)
            for s in range(4):
                fwd(s)
                yield
            yield
            yield
            for q in range(4):
                gbk = galloc(1)
                for s in range(4):
                    em.op("pe", lambda s=s, q=q, gbk=gbk: pe.transpose(out=G[:, gbk, s * 128:(s + 1) * 128], in_=cn[:, s, q * 128:(q + 1) * 128],
                                                                       identity=ident_f[:]),
                          reads=[("cn", s), "ident_f"], writes=[("G", gbk)], c=0.15)
                sl = q % 2
                em.op("act", lambda q=q, sl=sl, gbk=gbk: act.activation(out=s_ln[sl][:], in_=G[:, gbk, :], func=AF.Silu,
                                                                        bias=colp[:, C_LB + q:C_LB + q + 1], scale=colp[:, C_LG + q:C_LG + q + 1]),
                      reads=[("G", gbk), "colp"], writes=[("s_ln", sl)], tbl="silu")
                gfree(gbk)
                em.op("dve", lambda q=q, sl=sl: dve.tensor_tensor(out=y_all[:, q, st * ST:(st + 1) * ST], in0=s_ln[sl][:], in1=s_az[:, q, :], op=ALU.mult),
                      reads=[("s_ln", sl), ("s_az", q)], writes=[("ya", st, q)])
                if st + 1 < NST:
                    nh = (st + 1) % 2
                    az_chunk(q, nh, [("hT", nh, s) for s in range(4)])
                yield

        def B_P(st, s):
            hs = st % 2
            c0 = s * 128
            bs_ = s % 2
            hk = [("hT", hs, s)]
            order = (("z", LB_Z, 2), ("v", LB_V, 1), ("u", LB_U, 0)) if s % 2 == 0 else (("v", LB_V, 1), ("u", LB_U, 0), ("z", LB_Z, 2))
            dsts = {"z": (szb[bs_], ("szb", bs_), AF.Silu), "v": (gvb[bs_], ("gvb", bs_), AF.Gelu), "u": (ub[bs_], ("ub", bs_), AF.Gelu)}
            for name, off, kb in order:
                b = galloc(1)
                for k in range(8):
                    em.op("pe", lambda k=k, b=b, kb=kb: pe.matmul(G[:, b, :], lhsT=hT[hs][:, k, c0:c0 + 128], rhs=wB[:, kb, k, :],
                                                                    start=(k == 0), stop=(k == 7)),
                          reads=[("wB", kb)] + hk, writes=[("G", b)])
                dt_, dk_, fn_ = dsts[name]
                em.op("act", lambda b=b, dt_=dt_, fn_=fn_: act.activation(out=dt_[:], in_=G[:, b, :], func=fn_), reads=[("G", b)], writes=[dk_],
                      tbl=("silu" if name == "z" else "gelu"))
                gfree(b)
            sa, sk = tmp_stat()
            ln_stats([gvb[bs_][:]], [("gvb", bs_)], sa, sk)
            em.op("dve", lambda: dve.scalar_tensor_tensor(out=gvb[bs_][:], in0=gvb[bs_][:], scalar=sa[:, 0:1], in1=sg_b, op0=ALU.subtract, op1=ALU.mult),
                  reads=[("gvb", bs_), sk, "bc_s"], writes=[("gvb", bs_)])
            em.op("dve", lambda: dve.scalar_tensor_tensor(out=vb[bs_][:], in0=gvb[bs_][:], scalar=sa[:, 2:3], in1=sb_b, op0=ALU.mult, op1=ALU.add),
                  reads=[("gvb", bs_), sk, "bc_s"], writes=[("vb", bs_)])
            em.op("pool", lambda: pool.tensor_tensor(out=ub[bs_][:], in0=ub[bs_][:], in1=szb[bs_][:], op=ALU.mult),
                  reads=[("ub", bs_), ("szb", bs_)], writes=[("ub", bs_)])

        def B_Q(st, s):
            bs_ = s % 2
            gs = st * 4 + s
            bsg = galloc(1)
            for h in range(8):
                em.op("pe", lambda h=h: pe.matmul(G[:, bsg, h * 64:(h + 1) * 64], lhsT=wmT[:, h, :], rhs=vb[bs_][:, h * 64:(h + 1) * 64],
                                                  start=True, stop=False), reads=["wmT", ("vb", bs_)], writes=[("G", bsg)], c=0.03)
                em.op("pe", lambda h=h: pe.matmul(G[:, bsg, h * 64:(h + 1) * 64], lhsT=bs_rows[:, h * 128:(h + 1) * 128], rhs=ones2[:, 0:64],
                                                  start=False, stop=True), reads=["rows2", "ones2"], writes=[("G", bsg)], c=0.03)
            em.op("dve", lambda: dve.tensor_tensor(out=ybb[bs_][:], in0=G[:, bsg, :], in1=ub[bs_][:], op=ALU.mult),
                  reads=[("G", bsg), ("ub", bs_)], writes=[("ybb", bs_)])
            gfree(bsg)

        def B_Q2(st, s):
            bs_ = s % 2
            gs = st * 4 + s
            tb = tballoc()
            for e in range(4):
                em.op("pe", lambda e=e: pe.transpose(out=TB[:, tb, e * 128:(e + 1) * 128], in_=ybb[bs_][:, e * 128:(e + 1) * 128], identity=ident_b[:]),
                      reads=[("ybb", bs_), "ident_b"], writes=[("TB", tb)])
            em.op("act", lambda: act.copy(out=y_all[:, 4:8, gs * 128:(gs + 1) * 128], in_=TB[:, tb, 0:512].rearrange("p (e t) -> p e t", e=4)),
                  reads=[("TB", tb)], writes=[("yb", gs)])

        def gen_B(st):
            B_P(st, 0)
            yield
            B_P(st, 1)
            yield
            B_Q(st, 0)
            yield
            B_P(st, 2)
            yield
            B_Q2(st, 0)
            B_Q(st, 1)
            yield
            B_P(st, 3)
            yield
            B_Q2(st, 1)
            B_Q(st, 2)
            yield
            yield
            B_Q2(st, 2)
            B_Q(st, 3)
            yield
            yield
            B_Q2(st, 3)
            yield
            if st == NST - 1:
                o_wg = em.op("pool", lambda: pool.dma_start(out=w_gate[:, 0:2, :], in_=w_gate_d[:, 0:2, :]), writes=WB_ALL + ["w_gate"],
                             dma="w_gate", c=9.0)
                for kk in (2, 4, 6):
                    o_n = em.op("pool", lambda kk=kk: pool.dma_start(out=w_gate[:, kk:kk + 2, :], in_=w_gate_d[:, kk:kk + 2, :]), writes=["w_gate"],
                                dma="w_gate", c=9.0 + 2.1 * kk)
                    o_n.deps = list(o_wg.deps)

        P2 = {}
        P2V = int(os.environ.get("P2V", "0"))

        def p2_ctx(gs):
            if gs not in P2:
                sl = gs % R
                P2[gs] = dict(sl=sl, r0=gs * 128, xr=xr_r[sl], zb=zb_r[sl], pbu=pb_r[sl], pTt=pT_r[sl], h2b=h2bf_r[sl],
                              h2t=h2T_r[sl], gbf=gb_r[sl], pf=pbf[gs % 2], pfk=("pbf", gs % 2),
                              sa1=st1[:, gs + 1, :], sk1=("st1", gs + 1))
            return P2[gs]

        def p2_loads(gs):
            c = p2_ctx(gs)
            em.op("sp", lambda: sp.dma_start(out=c["xr"].ap, in_=x_d[c["r0"]:c["r0"] + 128, :]), writes=c["xr"].wkeys(), dma=f"xr{c['sl']}", n=1024)
            em.op("sp", lambda: sp.dma_start(out=c["pbu"].ap, in_=p_d[c["r0"]:c["r0"] + 128, :]), writes=c["pbu"].wkeys(), dma=f"pbuf{c['sl']}", n=256)

        def p2_ppath(gs):
            c = p2_ctx(gs)
            pf, pfk, pbu, pTt = c["pf"], c["pfk"], c["pbu"], c["pTt"]
            em.op("act", lambda: act.copy(out=pf[:], in_=pbu.ap), reads=[pbu.key], writes=[pfk], n=256)
            tb2 = tballoc()
            for j in range(2):
                em.op("pe", lambda j=j: pe.transpose(out=TB[:, tb2, j * 128:(j + 1) * 128], in_=pf[:, j * 128:(j + 1) * 128], identity=ident_b[:]),
                      reads=[pfk, "ident_b"], writes=[("TB", tb2)])
            em.op("act", lambda: act.copy(out=pTt.ap, in_=TB[:, tb2, 0:256]), reads=[("TB", tb2)], writes=pTt.wkeys(), n=256)

        def p2_h2T(gs):
            c = p2_ctx(gs)
            h2b, h2t = c["h2b"], c["h2t"]
            tb = tballoc()
            for k in range(8):
                em.op("pe", lambda k=k: pe.transpose(out=TB[:, tb, k * 128:(k + 1) * 128], in_=h2b.ap[:, k * 128:(k + 1) * 128], identity=ident_b[:]),
                      reads=[h2b.key, "ident_b"], writes=[("TB", tb)])
            em.op("act", lambda: act.copy(out=h2t.ap, in_=TB[:, tb, :]), reads=[("TB", tb)], writes=h2t.wkeys(), n=1024)

        def p2_xprep(gs):
            c = p2_ctx(gs)
            xr, sa1, sk1 = c["xr"], c["sa1"], c["sk1"]
            if P2V >= 1:
                em.op("act", lambda: act.activation(out=xr.ap, in_=xr.ap, func=AF.Identity, bias=sa1[:, 3:4], scale=sa1[:, 2:3]),
                      reads=[xr.key, sk1], writes=[xr.key], n=1024)
                em.op("pool", lambda: pool.tensor_tensor(out=xr.ap, in0=xr.ap, in1=ag_b, op=ALU.mult), reads=[xr.key, "bc_l"], writes=[xr.key], n=1024)
                return
            em.op("dve", lambda: dve.scalar_tensor_tensor(out=xr.ap, in0=xr.ap, scalar=sa1[:, 0:1], in1=ag_b, op0=ALU.subtract, op1=ALU.mult),
                  reads=[xr.key, sk1, "bc_l"], writes=[xr.key], n=1024)

        def p2_mix(gs):
            c = p2_ctx(gs)
            r0 = c["r0"]
            st = gs // 4
            yk = [("ya", st, q) for q in range(4)] + [("yb", gs)]
            gm = galloc(2)
            c["gm"] = gm
            for nb in range(2):
                for e in range(8):
                    em.op("pe", lambda nb=nb, e=e: pe.matmul(G[:, gm + nb, :], lhsT=y_all[:, e, r0:r0 + 128], rhs=w_out[:, e, nb * 512:(nb + 1) * 512],
                                                             start=(e == 0), stop=False),
                          reads=["w_out"] + yk, writes=[("G", gm + nb)])
                em.op("pe", lambda nb=nb: pe.matmul(G[:, gm + nb, :], lhsT=ones2[:], rhs=ab_rows[:, nb * 512:(nb + 1) * 512],
                                                    start=False, stop=True), reads=["ones2", "rows2"], writes=[("G", gm + nb)])

        def p2_ln(gs):
            c = p2_ctx(gs)
            xr, zb, h2b, sa1, sk1, gm = c["xr"], c["zb"], c["h2b"], c["sa1"], c["sk1"], c["gm"]
            if P2V >= 1:
                em.op("dve", lambda: dve.tensor_tensor(out=zb.ap.rearrange("p (a b) -> p a b", a=2), in0=xr.ap.rearrange("p (a b) -> p a b", a=2),
                                                       in1=G[:, gm:gm + 2, :], op=ALU.add),
                      reads=[xr.key, ("G", gm), ("G", gm + 1)], writes=zb.wkeys(), n=1024)
            else:
                em.op("dve", lambda: dve.scalar_tensor_tensor(out=zb.ap.rearrange("p (a b) -> p a b", a=2), in0=xr.ap.rearrange("p (a b) -> p a b", a=2),
                                                              scalar=sa1[:, 2:3], in1=G[:, gm:gm + 2, :], op0=ALU.mult, op1=ALU.add),
                      reads=[xr.key, sk1, ("G", gm), ("G", gm + 1)], writes=zb.wkeys(), n=1024)
            gfree(gm, gm + 1)
            sa, sk = tmp_stat()
            ln_stats([zb.ap[:, 0:512], zb.ap[:, 512:1024]], [zb.key], sa, sk, need_nmr=(P2V >= 2))
            if P2V >= 2:
                em.op("act", lambda: act.activation(out=zb.ap, in_=zb.ap, func=AF.Identity, bias=sa[:, 3:4], scale=sa[:, 2:3]),
                      reads=[zb.key, sk], writes=[zb.key], n=1024)
                em.op("pool", lambda: pool.tensor_tensor(out=zb.ap, in0=zb.ap, in1=pg_b, op=ALU.mult), reads=[zb.key, "bc_l"], writes=[zb.key], n=1024)
                em.op("dve", lambda: dve.tensor_tensor(out=zb.ap, in0=zb.ap, in1=pb_b, op=ALU.add), reads=[zb.key, "bc_l"], writes=[zb.key], n=1024)
            else:
                em.op("dve", lambda: dve.scalar_tensor_tensor(out=zb.ap, in0=zb.ap, scalar=sa[:, 0:1], in1=pg_b, op0=ALU.subtract, op1=ALU.mult),
                      reads=[zb.key, sk, "bc_l"], writes=[zb.key], n=1024)
                em.op("dve", lambda: dve.scalar_tensor_tensor(out=zb.ap, in0=zb.ap, scalar=sa[:, 2:3], in1=pb_b, op0=ALU.mult, op1=ALU.add),
                      reads=[zb.key, sk, "bc_l"], writes=[zb.key], n=1024)
            em.op("act", lambda: act.copy(out=h2b.ap, in_=zb.ap), reads=[zb.key], writes=h2b.wkeys(), n=1024)

        def p2_gate(gs):
            c = p2_ctx(gs)
            pTt, h2t, gbf = c["pTt"], c["h2t"], c["gbf"]
            gp = galloc(2)
            for nb in range(2):
                for j in range(2):
                    em.op("pe", lambda nb=nb, j=j: pe.matmul(G[:, gp + nb, :], lhsT=pTt.ap[:, j * 128:(j + 1) * 128], rhs=w_ple[:, j, nb * 512:(nb + 1) * 512],
                                                             start=(j == 0), stop=(j == 1)),
                          reads=["w_ple", pTt.key], writes=[("G", gp + nb)])
            gg = galloc(2)
            for nb in range(2):
                for k in range(8):
                    em.op("pe", lambda nb=nb, k=k: pe.matmul(G[:, gg + nb, :], lhsT=h2t.ap[:, k * 128:(k + 1) * 128], rhs=w_gate[:, k, nb * 512:(nb + 1) * 512],
                                                             start=(k == 0), stop=False),
                          reads=["w_gate", h2t.key], writes=[("G", gg + nb)])
                em.op("pe", lambda nb=nb: pe.matmul(G[:, gg + nb, :], lhsT=ones2[:], rhs=bg_rows[:, nb * 512:(nb + 1) * 512],
                                                    start=False, stop=True), reads=["ones2", "rows2"], writes=[("G", gg + nb)])
            g3 = gbf.ap.rearrange("p (a b) -> p a b", a=2)
            em.op("act", lambda: act.activation(out=g3, in_=G[:, gg:gg + 2, :], func=AF.Tanh, scale=0.5),
                  reads=[("G", gg), ("G", gg + 1)], writes=gbf.wkeys(), n=1024)
            gfree(gg, gg + 1)
            em.op("dve", lambda: dve.scalar_tensor_tensor(out=g3, in0=g3, scalar=1.0, in1=G[:, gp:gp + 2, :], op0=ALU.add, op1=ALU.mult),
                  reads=[gbf.key, ("G", gp), ("G", gp + 1)], writes=[gbf.key], n=1024)
            gfree(gp, gp + 1)

        def p2_out(gs):
            c = p2_ctx(gs)
            gbf, zb, r0 = c["gbf"], c["zb"], c["r0"]
            em.op("pool", lambda: pool.tensor_tensor(out=gbf.ap, in0=gbf.ap, in1=zb.ap, op=ALU.add), reads=[gbf.key, zb.key], writes=[gbf.key], n=1024)
            return em.op("sp", lambda: sp.dma_start(out=out_d[r0:r0 + 128, :], in_=gbf.ap), reads=[gbf.key], writes=[gbf.key], dma=f"out{c['sl']}", n=1024)

        load_small()
        _interleave([gen_S1(0)])
        weights_rest()
        s1_load(xh_d, 0, 0)
        s1_trans(0, 1, 0)
        setup_late()
        for q in range(4):
            glu_chunk(q, 128, 1, [("hT", 1, 0)], a_buf[0][:, q, 0:32], ("a_buf", 0, q), 96)
        em.op("dve", lambda: dve.tensor_scalar(out=a_buf[0][:, :, 0:32], in0=a_buf[0][:, :, 0:32], scalar1=colp[:, C_HM:C_HM + 1], scalar2=None,
                                               op0=ALU.mult),
              reads=[("a_buf", 0, q) for q in range(4)] + ["colp"], writes=[("a_buf", 0, q) for q in range(4)])
        for q in range(4):
            az_chunk(q, 0, [("hT", 0, s) for s in range(4)])
        for st in range(NST):
            _interleave([gen_Ahead(st), gen_B(st), gen_Atail(st - 1) if st >= 1 else None,
                         gen_S1(st + 1) if st + 1 < NST else None])
            if st + 1 < NST:
                head_copy(st)
            if st == 1:
                em.op("sp", lambda: sp.dma_start(out=bc[:, 0:3072], in_=bc_d[:, 0:3072]), reads=[("ya", 0, 0)], writes=["bc_l"], dma="bc_l", c=10.0)
                em.op("dve", lambda: dve.tensor_scalar(out=ag_b, in0=ag_b, scalar1=ALPHA, scalar2=None, op0=ALU.mult),
                      reads=["bc_l"], writes=["bc_l"], n=1024)
        emit_pass2_wA()
        _interleave([gen_Atail(NST - 1)])
        last_out = []
        em.op("dve", lambda: dve.tensor_scalar(out=w_ple, in0=w_ple, scalar1=0.5, scalar2=None, op0=ALU.mult),
              reads=["w_ple"], writes=["w_ple"])
        ok = lambda i: 0 <= i < NSUB
        p2_loads(0)
        for t in range(NSUB + 3):
            if ok(t - 3):
                last_out.append(p2_out(t - 3))
            if ok(t + 1):
                p2_loads(t + 1)
            if ok(t):
                p2_ppath(t)
            if ok(t - 2):
                p2_h2T(t - 2)
            if ok(t):
                p2_xprep(t)
            if ok(t - 1):
                p2_ln(t - 1)
            if ok(t):
                p2_mix(t)
            if ok(t - 2):
                p2_gate(t - 2)
        nwait = em.finalize(es, final=last_out)
        import os
        if os.environ.get("KVERB"):
            print("sched estimate us:", getattr(em, "est", None), "waits:", nwait, "ops:", len(em.ops))
    return nc


def _perm():
    perm = np.zeros(512, dtype=np.int64)
    for qo in range(4):
        for j in range(4):
            for r in range(32):
                perm[qo * 128 + 32 * j + r] = j * 128 + 32 * qo + r
    return perm


_NC_CACHE = {}


def kernel(x, p, ln_emb_g, ln_emb_b, w_in, conv_w, conv_b, conv_ln_g, conv_ln_b, sgu_ln_g, sgu_ln_b, w_s, b_s,
           w_out, post_ln_g, post_ln_b, w_ple, w_ple_gate, b_ple_gate):
    f = lambda a: np.ascontiguousarray(np.asarray(a, dtype=np.float32))
    x, p = f(x), f(p)
    perm = _perm()
    w_in0 = f(w_in)[0]
    colsA = np.concatenate([np.arange(0, 1024), A_Z + perm])
    colsB = np.arange(1536, 3072)
    w_inA_l = np.ascontiguousarray(w_in0[:, colsA].reshape(8, 128, 12, 128).transpose(1, 2, 0, 3))
    w_inB_l = np.ascontiguousarray(w_in0[:, colsB].reshape(8, 128, 3, 512).transpose(1, 2, 0, 3))
    w_out0 = f(w_out)[0]
    rowsel = np.arange(1024)
    rowsel[0:512] = perm
    w_out_l = np.ascontiguousarray(w_out0[rowsel, :].reshape(8, 128, 1024).transpose(1, 0, 2))
    w_gate_l = np.ascontiguousarray(f(w_ple_gate)[0].reshape(8, 128, 1024).transpose(1, 0, 2))
    w_ple_l = np.ascontiguousarray(f(w_ple)[0].reshape(2, 128, 1024).transpose(1, 0, 2))
    wsT = np.ascontiguousarray(f(w_s)[0].transpose(2, 0, 1))
    cw = np.ascontiguousarray(f(conv_w)[0].T.reshape(4, 128, NTAP).transpose(1, 0, 2))
    colp = np.zeros((128, NCOL), np.float32)
    colp[:, C_GE:C_GE + 8] = f(ln_emb_g).reshape(8, 128).T
    colp[:, C_BE:C_BE + 8] = f(ln_emb_b).reshape(8, 128).T
    colp[:, C_CB:C_CB + 4] = f(conv_b)[0][perm].reshape(4, 128).T
    colp[:, C_LG:C_LG + 4] = f(conv_ln_g)[0][perm].reshape(4, 128).T
    colp[:, C_LB:C_LB + 4] = f(conv_ln_b)[0][perm].reshape(4, 128).T
    colp[0, C_M0] = 1.0
    colp[1, C_M1] = 1.0
    bcrow = np.concatenate([f(ln_emb_g), f(post_ln_g)[0], f(post_ln_b)[0], f(sgu_ln_g)[0], f(sgu_ln_b)[0]])
    bc = np.ascontiguousarray(np.broadcast_to(bcrow[None, :], (128, 4096)))
    rrow = np.concatenate([f(ln_emb_b), f(b_ple_gate)[0], f(b_s)[0].reshape(-1)])
    rows = np.ascontiguousarray(np.broadcast_to(rrow[None, :], (2, 3072)))

    B, S, _ = x.shape
    per_b = S // NT
    in_maps = []
    for c in range(NCORES):
        b, j = divmod(c, per_b)
        t0 = j * NT
        xc = x[b, t0:t0 + NT]
        pc = p[0, b, t0:t0 + NT]
        cp = colp.copy()
        if j == 0:
            xh = np.zeros((128, D), np.float32)
            cp[:, C_HM] = 0.0
        else:
            xh = x[b, t0 - 128:t0]
            cp[:, C_HM] = 1.0
        in_maps.append({"x": np.ascontiguousarray(xc), "xh": np.ascontiguousarray(xh), "p": np.ascontiguousarray(pc),
                        "w_inA": w_inA_l, "w_inB": w_inB_l, "w_out": w_out_l, "w_gate": w_gate_l, "w_ple": w_ple_l, "wsT": wsT, "cw": cw,
                        "colp": cp, "bc": bc, "rows": rows})
    nc = build_program()
    res = run_bass_kernel_spmd(nc, in_maps, core_ids=list(range(NCORES)))
    out = np.empty((B, S, D), np.float32)
    for c in range(NCORES):
        b, j = divmod(c, per_b)
        out[b, j * NT:(j + 1) * NT] = res.results[c]["out"]
    return out
```
